# Optimizing a Trainium2 kernel written in Bass

```python
import math
import jax, jax.numpy as jnp
from jax import lax
import numpy as np

D_MODEL = 1024
BATCH = 32
SEQ = 2048
DEPTH = 1

HEAD_DIM = 64
HEADS_PER_GROUP = 8
DILATED_GROUPS = ((128, 1), (512, 4), (2048, 16))
N_GROUPS = len(DILATED_GROUPS)
N_ATTN_HEADS = N_GROUPS * HEADS_PER_GROUP
ATTN_QKV = N_ATTN_HEADS * HEAD_DIM
ATTN_OUT = HEADS_PER_GROUP * HEAD_DIM
BLOCK = 128
NEG = -1e30
NUM_BUCKETS = 32
MAX_EXACT = NUM_BUCKETS // 2
MAX_DISTANCE = 2048
LRU_WIDTH = D_MODEL
LRU_BLOCKS = 16
LRU_BLOCK_W = LRU_WIDTH // LRU_BLOCKS
LRU_CONV = 4
C_RGLRU = 8.0
D_FF = 3 * D_MODEL
FFN_CONV = 3
IN_COLS = 3 * ATTN_QKV + 2 * LRU_WIDTH + 2 * D_MODEL
ALPHA = (2.0 * DEPTH) ** 0.25
BETA = (8.0 * DEPTH) ** -0.25
LN_EPS = 1e-5

kernel_name = "hybrid_dilated_attn_rglru_convffn_block"


def layer_norm(x, g, b):
    xf = x.astype(jnp.float32)
    mu = xf.mean(-1, keepdims=True)
    var = jnp.square(xf - mu).mean(-1, keepdims=True)
    return ((xf - mu) * lax.rsqrt(var + LN_EPS) * g + b).astype(x.dtype)


def causal_dwconv(x, w, b):
    K = w.shape[0]
    S = x.shape[1]
    xp = jnp.pad(x, ((0, 0), (K - 1, 0), (0, 0)))
    y = b
    for k in range(K):
        y = y + xp[:, k:k + S] * w[k]
    return y


def rel_bucket(dist):
    is_small = dist < MAX_EXACT
    nf = jnp.maximum(dist, 1).astype(jnp.float32)
    large = MAX_EXACT + (jnp.log(nf / MAX_EXACT) / math.log(MAX_DISTANCE / MAX_EXACT)
                         * (NUM_BUCKETS - MAX_EXACT)).astype(jnp.int32)
    large = jnp.minimum(large, NUM_BUCKETS - 1)
    return jnp.where(is_small, dist, large)


def dilated_window_attention(q, k, v, bias_table, window, dilation):
    B, S, H, Dh = q.shape
    span = dilation * BLOCK
    Sp = -(-S // span) * span
    L = Sp // dilation
    nblk = L // BLOCK

    def to_blocks(t):
        t = jnp.pad(t, ((0, 0), (0, Sp - S), (0, 0), (0, 0)))
        t = t.reshape(B, L, dilation, H, Dh).transpose(0, 2, 3, 1, 4)
        return t.reshape(B, dilation, H, nblk, BLOCK, Dh)

    def with_prev(t):
        prev = jnp.pad(t, ((0, 0), (0, 0), (0, 0), (1, 0), (0, 0), (0, 0)))[:, :, :, :-1]
        return jnp.concatenate([prev, t], axis=4)

    qb = to_blocks(q * (HEAD_DIM ** -0.5))
    kk = with_prev(to_blocks(k))
    vv = with_prev(to_blocks(v))
    logits = jnp.einsum('brhnqc,brhnkc->brhnqk', qb, kk).astype(jnp.float32)

    qi = jnp.arange(BLOCK)[:, None]
    kj = jnp.arange(2 * BLOCK)[None, :]
    dist = qi + BLOCK - kj
    in_band = (dist >= 0) & (dist <= window // dilation)
    bucket = rel_bucket(jnp.maximum(dist, 0) * dilation)
    bias = bias_table[bucket].astype(jnp.float32).transpose(2, 0, 1)
    blk = jnp.arange(nblk)[:, None, None]
    mask = in_band[None] & ((blk > 0) | (kj >= BLOCK)[None])

    logits = jnp.where(mask, logits + bias[:, None], NEG)
    m = logits.max(-1, keepdims=True)
    p = jnp.exp(logits - m)
    s = p.sum(-1, keepdims=True)
    o = jnp.einsum('brhnqk,brhnkc->brhnqc', p, vv.astype(jnp.float32)) / s
    lse = (m + jnp.log(s))[..., 0]
    o = o.reshape(B, dilation, H, L, Dh).transpose(0, 3, 1, 2, 4).reshape(B, Sp, H, Dh)[:, :S]
    lse = lse.reshape(B, dilation, H, L).transpose(0, 3, 1, 2).reshape(B, Sp, H)[:, :S]
    return o, lse


def rg_lru(xr, wa, ba, wx, bx, lam):
    B, S, W = xr.shape
    xf = xr.astype(jnp.float32)
    xb = xf.reshape(B, S, LRU_BLOCKS, LRU_BLOCK_W)
    ba_b = ba.astype(jnp.float32).reshape(LRU_BLOCKS, LRU_BLOCK_W)
    bx_b = bx.astype(jnp.float32).reshape(LRU_BLOCKS, LRU_BLOCK_W)
    r = jax.nn.sigmoid(jnp.einsum('bsnc,ncd->bsnd', xb, wa.astype(jnp.float32)) + ba_b).reshape(B, S, W)
    i = jax.nn.sigmoid(jnp.einsum('bsnc,ncd->bsnd', xb, wx.astype(jnp.float32)) + bx_b).reshape(B, S, W)
    log_a = -C_RGLRU * r * jax.nn.softplus(-lam.astype(jnp.float32))
    a = jnp.exp(log_a)
    b = jnp.sqrt(-jnp.expm1(2.0 * log_a)) * (i * xf)

    def combine(left, right):
        a1, b1 = left
        a2, b2 = right
        return a1 * a2, a2 * b1 + b2

    _, h = lax.associative_scan(combine, (a, b), axis=1)
    return h.astype(xr.dtype)


def setup_inputs(seed: int = 0) -> dict:
    key = jax.random.key(seed)
    ks = jax.random.split(key, 24)
    f32 = jnp.float32
    nrm = lambda k, shape, scale: jax.random.normal(k, shape, f32) * scale
    D = D_MODEL
    w_in = nrm(ks[2], (DEPTH, D, IN_COLS), D ** -0.5)
    w_in = w_in.at[:, :, 2 * ATTN_QKV:3 * ATTN_QKV].multiply(BETA)
    a_c = jax.random.uniform(ks[10], (DEPTH, LRU_WIDTH), f32, 0.9, 0.999)
    a0 = a_c ** (1.0 / C_RGLRU)
    lam = jnp.log(a0) - jnp.log1p(-a0)
    return {
        "x": nrm(ks[0], (BATCH, SEQ, D), 1.0),
        "c": nrm(ks[1], (BATCH, D), 1.0),
        "w_ada": nrm(ks[3], (DEPTH, D, 6 * D), D ** -0.5),
        "b_ada": nrm(ks[4], (DEPTH, 6 * D), 0.01),
        "w_in": w_in,
        "rel_bias": nrm(ks[5], (NUM_BUCKETS, N_ATTN_HEADS), 0.5),
        "lru_conv_w": nrm(ks[6], (DEPTH, LRU_CONV, LRU_WIDTH), LRU_CONV ** -0.5),
        "lru_conv_b": nrm(ks[7], (DEPTH, LRU_WIDTH), 0.01),
        "lru_wa": nrm(ks[8], (DEPTH, LRU_BLOCKS, LRU_BLOCK_W, LRU_BLOCK_W), LRU_BLOCK_W ** -0.5),
        "lru_ba": nrm(ks[9], (DEPTH, LRU_WIDTH), 0.01),
        "lru_wx": nrm(ks[11], (DEPTH, LRU_BLOCKS, LRU_BLOCK_W, LRU_BLOCK_W), LRU_BLOCK_W ** -0.5),
        "lru_bx": nrm(ks[12], (DEPTH, LRU_WIDTH), 0.01),
        "lru_lambda": lam,
        "w_proj_attn": nrm(ks[13], (DEPTH, ATTN_OUT, D), BETA * ATTN_OUT ** -0.5),
        "w_proj_lru": nrm(ks[14], (DEPTH, LRU_WIDTH, D), BETA * LRU_WIDTH ** -0.5),
        "w_out": nrm(ks[15], (DEPTH, D, D), BETA * D ** -0.5),
        "ln1_g": 1.0 + nrm(ks[16], (DEPTH, D), 0.02),
        "ln1_b": nrm(ks[17], (DEPTH, D), 0.02),
        "ffn_w_up": nrm(ks[18], (DEPTH, D, 2 * D_FF), D ** -0.5),
        "ffn_conv_w": nrm(ks[19], (DEPTH, FFN_CONV, 2 * D_FF), FFN_CONV ** -0.5),
        "ffn_conv_b": nrm(ks[20], (DEPTH, 2 * D_FF), 0.01),
        "ffn_w_down": nrm(ks[21], (DEPTH, D_FF, D), BETA * D_FF ** -0.5),
        "ln2_g": 1.0 + nrm(ks[22], (DEPTH, D), 0.02),
        "ln2_b": nrm(ks[23], (DEPTH, D), 0.02),
    }


def reference(x, c, w_ada, b_ada, w_in, rel_bias, lru_conv_w, lru_conv_b, lru_wa, lru_ba,
              lru_wx, lru_bx, lru_lambda, w_proj_attn, w_proj_lru, w_out, ln1_g, ln1_b,
              ffn_w_up, ffn_conv_w, ffn_conv_b, ffn_w_down, ln2_g, ln2_b):
    B, S, D = x.shape
    c_act = jax.nn.silu(c)
    split_idx = [ATTN_QKV, 2 * ATTN_QKV, 3 * ATTN_QKV, 3 * ATTN_QKV + LRU_WIDTH,
                 3 * ATTN_QKV + 2 * LRU_WIDTH, 3 * ATTN_QKV + 2 * LRU_WIDTH + D_MODEL]
    for l in range(DEPTH):
        mod = c_act @ w_ada[l] + b_ada[l]
        sh1, sc1, g1, sh2, sc2, g2 = [t[:, None, :] for t in jnp.split(mod, 6, axis=-1)]

        u = x * (1.0 + sc1) + sh1
        proj = u @ w_in[l]
        q, k, v, x_lru, lru_gate, gate_a, gate_r = jnp.split(proj, split_idx, axis=-1)
        q = q.reshape(B, S, N_GROUPS, HEADS_PER_GROUP, HEAD_DIM)
        k = k.reshape(B, S, N_GROUPS, HEADS_PER_GROUP, HEAD_DIM)
        v = v.reshape(B, S, N_GROUPS, HEADS_PER_GROUP, HEAD_DIM)
        outs, lses = [], []
        for g, (window, dilation) in enumerate(DILATED_GROUPS):
            o_g, lse_g = dilated_window_attention(
                q[:, :, g], k[:, :, g], v[:, :, g],
                rel_bias[:, g * HEADS_PER_GROUP:(g + 1) * HEADS_PER_GROUP], window, dilation)
            outs.append(o_g)
            lses.append(lse_g)
        wts = jax.nn.softmax(jnp.stack(lses, 0), axis=0)
        o_attn = jnp.sum(wts[..., None] * jnp.stack(outs, 0), axis=0)
        y_attn = o_attn.reshape(B, S, ATTN_OUT).astype(x.dtype) @ w_proj_attn[l]

        xr = causal_dwconv(x_lru, lru_conv_w[l], lru_conv_b[l])
        h = rg_lru(xr, lru_wa[l], lru_ba[l], lru_wx[l], lru_bx[l], lru_lambda[l])
        y_lru = (h * jax.nn.gelu(lru_gate)) @ w_proj_lru[l]

        merged = jax.nn.sigmoid(gate_a) * y_attn + jax.nn.sigmoid(gate_r) * y_lru
        mix_out = merged @ w_out[l]
        x = layer_norm(ALPHA * x + g1 * mix_out, ln1_g[l], ln1_b[l])

        u2 = x * (1.0 + sc2) + sh2
        hff = causal_dwconv(u2 @ ffn_w_up[l], ffn_conv_w[l], ffn_conv_b[l])
        val, gt = jnp.split(hff, 2, axis=-1)
        ffn_out = (jax.nn.gelu(gt) * val) @ ffn_w_down[l]
        x = layer_norm(ALPHA * x + g2 * ffn_out, ln2_g[l], ln2_b[l])
    return x
```

```python
import math
import numpy as np
from contextlib import ExitStack
import concourse.bass as bass
import concourse.mybir as mybir
from concourse.bass_utils import run_bass_kernel_spmd

F32 = mybir.dt.float32
BF16 = mybir.dt.bfloat16
AF = mybir.ActivationFunctionType
ALU = mybir.AluOpType

P = 128
D = 1024
KC = 8
S = 2048
T = 512
NW = S // T
H = 8
HP = 4
NBLK = 41
NWB = 3
ALPHA = 2.0 ** 0.25
LN_EPS = 1e-5
NEG = -30000.0
NDS = 28
N_CORES = 8
SEQ_PER_CORE = 4

PV_LCW = 0
PV_LCB = 32
PV_BA = 40
PV_BX = 48
PV_LAM = 56
PV_FCW = 64
PV_FCB = 208
PV_N = 256


class Buf:
    __slots__ = ("name", "w", "rs", "excl", "wreal")

    def __init__(self, name):
        self.name = name
        self.w = None
        self.rs = []
        self.excl = name.startswith("bank")
        self.wreal = None


class Op:
    __slots__ = ("idx", "eng", "fn", "deps", "signal", "semval", "dma", "dsem", "dval")


COMPUTE = ("pe", "act", "dve", "pool")


class Prog:
    def __init__(self):
        self.ops = []
        self.bufs = {}
        self.dma_cnt = {}
        self.dsem_last = {}

    def B(self, name):
        b = self.bufs.get(name)
        if b is None:
            b = self.bufs[name] = Buf(name)
        return b

    def add(self, eng, fn, reads=(), writes=(), dma=False):
        op = Op()
        op.idx = len(self.ops)
        op.eng = eng
        op.fn = fn
        op.dma = dma
        op.signal = False
        op.semval = 0
        op.dsem = -1
        op.dval = 0
        deps = {}
        rb = [self.B(r) if isinstance(r, str) else r for r in reads]
        wb = [self.B(w) if isinstance(w, str) else w for w in writes]
        for b in rb:
            if b.excl:
                if b.wreal is not None:
                    deps[b.wreal] = True
                if b.w is not None:
                    deps.setdefault(b.w, False)
                for r in b.rs:
                    deps.setdefault(r, False)
            elif b.w is not None:
                deps[b.w] = True
        for b in wb:
            if b.w is not None:
                deps.setdefault(b.w, False)
            for r in b.rs:
                deps.setdefault(r, False)
        if dma:
            base, n_ = {"sp": (0, 16), "pool": (16, 12)}[eng]
            k_ = self.dma_cnt.get(eng, 0)
            self.dma_cnt[eng] = k_ + 1
            s = base + k_ % n_
            prev = self.dsem_last.get(s)
            if prev is not None:
                deps.setdefault(prev.idx, False)
                op.dval = prev.dval + 16
            else:
                op.dval = 16
            op.dsem = s
            self.dsem_last[s] = op
        final = []
        for di, raw in deps.items():
            if di == op.idx:
                continue
            dop = self.ops[di]
            if dop.dma:
                final.append(di)
            elif dma:
                final.append(di)
                dop.signal = True
            elif dop.eng == eng:
                if eng != "pe" and raw:
                    final.append(di)
                    dop.signal = True
            else:
                final.append(di)
                dop.signal = True
        op.deps = sorted(final)
        for b in rb:
            if b.excl:
                b.w = op.idx
                b.rs = []
            else:
                b.rs.append(op.idx)
        for b in wb:
            b.w = op.idx
            b.wreal = op.idx
            b.rs = []
        self.ops.append(op)
        return op

    def handoff(self, arena_names, new_names):
        acc = set()
        for n in arena_names:
            b = self.B(n)
            if b.w is not None:
                acc.add(b.w)
            acc.update(b.rs)
        for n in new_names:
            b = self.B(n)
            b.rs = sorted(acc | set(b.rs))

    def check_deadlock(self):
        queues = {}
        for op in self.ops:
            queues.setdefault(op.eng, []).append(op)
        pos = {k: 0 for k in queues}
        done = set()
        progress = True
        while progress:
            progress = False
            for k, q in queues.items():
                while pos[k] < len(q):
                    op = q[pos[k]]
                    if all(d in done for d in op.deps):
                        done.add(op.idx)
                        pos[k] += 1
                        progress = True
                    else:
                        break
        stuck = {k: (q[pos[k]].idx, [d for d in q[pos[k]].deps if d not in done]) for k, q in queues.items() if pos[k] < len(q)}
        if stuck:
            raise RuntimeError("deadlock in program order: %r" % (stuck,))

    def emit(self, nc, es):
        self.check_deadlock()
        engs = {"pe": nc.tensor, "act": nc.scalar, "dve": nc.vector, "pool": nc.gpsimd, "sp": nc.sync}
        sems = {k: es.enter_context(nc.semaphore("s_" + k)) for k in COMPUTE}
        dsems = [es.enter_context(nc.semaphore("d_%d" % i)) for i in range(NDS)]
        cnt = {k: 0 for k in COMPUTE}
        for op in self.ops:
            if not op.dma and op.signal:
                cnt[op.eng] += 1
                op.semval = cnt[op.eng]
        known = {k: {} for k in engs}
        snap = {}
        for op in self.ops:
            e = engs[op.eng]
            kn = known[op.eng]
            need = {}
            for di in op.deps:
                dop = self.ops[di]
                if dop.dma:
                    key = ("d", dop.dsem)
                    val = dop.dval
                    sem = dsems[dop.dsem]
                else:
                    key = dop.eng
                    val = dop.semval
                    sem = sems[dop.eng]
                if kn.get(key, 0) >= val:
                    continue
                cur = need.get(key)
                if cur is None or cur[0] < val:
                    need[key] = (val, sem, di)
            for key, (val, sem, di) in need.items():
                if kn.get(key, 0) >= val:
                    continue
                e.wait_ge(sem, val)
                kn[key] = val
                sn = snap.get(di)
                if sn:
                    for k2, v2 in sn.items():
                        if kn.get(k2, 0) < v2:
                            kn[k2] = v2
            ins = op.fn(e) if op.fn is not None else None
            if op.dma:
                ins.then_inc(dsems[op.dsem], 16)
                snap[op.idx] = dict(kn)
            elif op.signal:
                ins.then_inc(sems[op.eng], 1)
                sn = dict(kn)
                snap[op.idx] = sn


def build_program(nseq):
    nc = bass.Bass("TRN2", target_bir_lowering=False)
    NTOK = nseq * S

    def din(name, shape, dt=F32):
        return nc.dram_tensor(name, list(shape), dt, kind="ExternalInput").ap()

    x_d = din("x", [NTOK, D])
    cT_d = din("cT", [P, KC, nseq])
    wada_d = din("wada", [12, P, KC, 512])
    bada_d = din("bada", [1, 6144])
    wall_d = din("wall", [NBLK, P, KC, 512])
    tabs_d = din("tabs", [P, 40 * 128])
    pvec_d = din("pvec", [P, PV_N])
    lruw_d = din("lruw", [P, 8 * 2 * 128])
    lnp_d = din("lnp", [1, 4 * D])
    y_d = nc.dram_tensor("y", [NTOK, D], F32, kind="ExternalOutput").ap()
    wsc_d = nc.dram_tensor("wsc", [NBLK, P, KC, 512], BF16, kind="Internal").ap()
    gsc_d = nc.dram_tensor("gsc", [max(nseq, 2), 2048], F32, kind="Internal").ap()

    pg = Prog()
    es = ExitStack()
    with es:
        def sb(name, shape, dt=F32):
            return es.enter_context(nc.sbuf_tensor("sb_" + name, list(shape), dt))

        ident = sb("ident", [P, P])
        ones_bf = sb("ones_bf", [P, 64], BF16)
        pvec = sb("pvec", [P, PV_N])
        lruw = sb("lruw", [P, 8, 2, 128], BF16)
        tabs = sb("tabs", [P, 40, 128], BF16)
        lnbc = sb("lnbc", [P, 4, D])
        g12b = sb("g12b", [P, 2, D])
        modfm = sb("modfm", [P, 32, nseq])
        lrup = sb("lrup", [P, 4, 8])
        hist_l = sb("hist_l", [P, 8, 3])
        hstate = sb("hstate", [P, 8])
        hist_f = sb("hist_f", [P, 48, 2])
        K0 = sb("K0", [P, HP, 2, T], BF16)
        K1 = sb("K1", [P, HP, 2, T], BF16)
        K2 = sb("K2", [P, HP, 16, 128], BF16)
        V0 = sb("V0", [P, 2, 4, 512], BF16)
        V1 = sb("V1", [P, 2, 4, 512], BF16)
        V2 = sb("V2", [P, 16, 512], BF16)
        x1tok = sb("x1tok", [P, 4, D])
        uT = sb("uT", [P, KC, T], BF16)
        wbuf = sb("wbuf", [P, NWB, KC, 512], BF16)
        oaT = sb("oaT", [P, HP, T], BF16)
        stats = sb("stats", [P, 2, 6])
        mv = sb("mv", [P, 4])
        ztmp = sb("ztmp", [P, 2, 512])
        corr = sb("corr", [P, 48, 2])
        xst = sb("xst", [P, 2, D])
        ctmp = sb("ctmp", [P, 48])
        ARENA = 19456
        arena = sb("arena", [P, ARENA], BF16)
        banks = [es.enter_context(nc.psum_tensor("bank%d" % i, [P, 512], F32)) for i in range(8)]

        def bk(i):
            return banks[i % 8], "bank%d" % (i % 8)

        def av(off, n, dt=BF16, shape=None):
            v = arena[:, off:off + n]
            if dt == F32:
                v = v.bitcast(F32)
            return v

        a_qT = [av(g * 2048, 2048).rearrange("p (h t) -> p h t", h=HP) for g in range(3)]
        a_pT = [av(6144 + i * 512, 512) for i in range(4)] + [av(13312, 512)]
        a_lt = [av(8192 + i * 1024, 1024, F32) for i in range(2)] + [av(13824 + i * 1024, 1024, F32) for i in range(2)]
        a_rec = av(10240, 1024, F32)
        a_v2st = av(11264, 2048).rearrange("p (c n) -> p c n", c=4)
        l_f32 = [av(i * 1024, 1024, F32) for i in range(14)]
        l_xrb = [av(14336 + i * 512, 512) for i in range(2)]
        l_hgT = av(15360, 4096).rearrange("p (c t) -> p c t", c=KC)
        m_sig = [av(4096 + i * 1024, 1024, F32) for i in range(3)]
        m_mT = av(0, 4096).rearrange("p (c t) -> p c t", c=KC)
        f_gT = av(0, 12288).rearrange("p (c t) -> p c t", c=24)
        f_tmp = [av(12288 + i * 1024, 1024, F32) for i in range(6)]
        s_stage = av(0, 8192, F32).rearrange("p (k n) -> p k n", k=KC)
        M_BUFS = ["q%d_%d" % (g, hp) for g in range(3) for hp in range(HP)] + ["pT0", "pT1", "pT2", "pT3", "pT4", "lt0", "lt1", "lt2", "lt3", "rec", "v2st0", "v2st1", "v2st2", "v2st3"]
        L_BUFS = ["%s%d" % (n_, i_) for n_ in ("xr", "th_r", "th_i", "a_", "s_", "hh", "gel", "xrb") for i_ in range(2)] + ["hgT%d" % i_ for i_ in range(8)]
        E_BUFS = ["sA", "sR"] + ["mT%d" % i_ for i_ in range(8)]
        F_BUFS = ["gT%d" % i_ for i_ in range(24)] + ["%s%d" % (n_, i_) for n_ in ("cv", "cg", "gl") for i_ in range(2)]
        S_BUFS = ["sstage0", "sstage1", "mt0", "mt1"]
        ALL_A = M_BUFS + L_BUFS + E_BUFS + F_BUFS + S_BUFS

        ADD = pg.add

        ADD("sp", lambda e: e.dma_start(out=pvec[:], in_=pvec_d), [], ["pvec"], dma=True)
        ADD("sp", lambda e: e.dma_start(out=lnbc[:].rearrange("p a d -> p (a d)"),
                                        in_=lnp_d.partition_broadcast(P)),
            [], ["lnbc"], dma=True)
        ADD("pool", lambda e: e.dma_start(out=tabs[:].rearrange("p a q -> p (a q)"), in_=tabs_d), [], ["tabs"], dma=True)
        ADD("pool", lambda e: e.dma_start(out=lruw[:].rearrange("p a b c -> p (a b c)"), in_=lruw_d), [], ["lruw"], dma=True)
        ADD("pool", lambda e: e.memset(ident[:], 0.0), [], ["ident"])
        ADD("pool", lambda e: e.affine_select(out=ident[:], in_=ident[:], compare_op=ALU.not_equal, fill=1.0,
                                              base=0, pattern=[[-1, P]], channel_multiplier=1), ["ident"], ["ident"])
        ADD("pool", lambda e: e.memset(ones_bf[:], 1.0), [], ["ones"])
        for t_, nm in ((K0, "K0"), (K1, "K1"), (K2, "K2"), (V0, "V0"), (V1, "V1"), (V2, "V2")):
            ADD("pool", (lambda e, t_=t_: e.memset(t_[:], 0.0)), [], [nm + "_all"])
        ADD("act", lambda e: e.activation(out=lrup[:, 0, :], in_=pvec[:, PV_LAM:PV_LAM + 8], func=AF.Exp, scale=-1.0),
            ["pvec"], ["lrup"])
        ADD("act", lambda e: e.activation(out=lrup[:, 0, :], in_=lrup[:, 0, :], func=AF.Ln, bias=1.0), ["lrup"], ["lrup"])
        ADD("dve", lambda e: e.tensor_scalar(out=lrup[:, 1, :], in0=lrup[:, 0, :], scalar1=-8.0, scalar2=0.0, op0=ALU.mult, op1=ALU.add),
            ["lrup"], ["lrup"])
        ADD("dve", lambda e: e.tensor_scalar(out=lrup[:, 0, :], in0=lrup[:, 1, :], scalar1=0.5, scalar2=0.0, op0=ALU.mult, op1=ALU.add),
            ["lrup"], ["lrup"])
        ADD("dve", lambda e: e.tensor_scalar(out=lrup[:, 2, :], in0=pvec[:, PV_BA:PV_BA + 8], scalar1=0.5, scalar2=0.0,
                                             op0=ALU.mult, op1=ALU.add), ["pvec", "lrup"], ["lrup"])
        ADD("dve", lambda e: e.tensor_scalar(out=lrup[:, 3, :], in0=pvec[:, PV_BX:PV_BX + 8], scalar1=0.5, scalar2=0.0,
                                             op0=ALU.mult, op1=ALU.add), ["pvec", "lrup"], ["lrup"])
        cact = ztmp[:, 0, 0:KC * nseq].rearrange("p (k b) -> p k b", k=KC)
        bb4 = ztmp[:, 1, :]
        ADD("sp", lambda e: e.dma_start(out=cact, in_=cT_d), [], ["zt0"], dma=True)
        ADD("act", lambda e: e.activation(out=cact, in_=cact, func=AF.Silu), ["zt0"], ["zt0"])
        stg = [av(i * 8192, 8192, F32).rearrange("p (k n) -> p k n", k=KC) for i in range(2)]
        mts = [av(16384 + i * 1024, 1024, F32) for i in range(2)]
        for blk in range(12):
            kind = blk // 2
            sst = stg[blk % 2]; sstn = "sstage%d" % (blk % 2)
            mt = mts[blk % 2]; mtn = "mt%d" % (blk % 2)
            ADD("pool", (lambda e, blk=blk, sst=sst: e.dma_start(out=sst, in_=wada_d[blk])), [], [sstn], dma=True)
            ADD("sp", (lambda e, blk=blk: e.dma_start(out=bb4[0:nseq, :],
                                                      in_=bada_d[0:1, blk * 512:(blk + 1) * 512].partition_broadcast(nseq))),
                [], ["zt1"], dma=True)
            pb_, pbn = bk(blk)
            for kc in range(KC):
                ADD("pe", (lambda e, kc=kc, pb_=pb_, sst=sst: e.matmul(pb_[0:nseq, :], lhsT=cact[:, kc, :], rhs=sst[:, kc, :],
                                                                       start=(kc == 0), stop=(kc == KC - 1))),
                    ["zt0", sstn], [pbn])
            ADD("dve", (lambda e, pb_=pb_, mt=mt: e.tensor_tensor(out=mt[0:nseq, :], in0=pb_[0:nseq, :], in1=bb4[0:nseq, :], op=ALU.add)),
                [pbn, "zt1"], [mtn])
            if kind in (2, 5):
                gi = 0 if kind == 2 else 1
                col = gi * 1024 + (blk % 2) * 512
                ADD("sp", (lambda e, col=col, mt=mt: e.dma_start(out=gsc_d[0:nseq, col:col + 512], in_=mt[0:nseq, :])),
                    [mtn], ["gsc"], dma=True)
            else:
                ki = {0: 0, 1: 1, 3: 2, 4: 3}[kind]
                if kind in (1, 4):
                    ADD("dve", (lambda e, mt=mt: e.tensor_scalar(out=mt[0:nseq, :], in0=mt[0:nseq, :], scalar1=1.0, scalar2=0.0,
                                                                 op0=ALU.add, op1=ALU.add)), [mtn], [mtn])
                tb_, tbn = bk(blk + 4)
                for q4 in range(4):
                    ADD("pe", (lambda e, q4=q4, tb_=tb_, mt=mt: e.transpose(out=tb_[:, q4 * nseq:(q4 + 1) * nseq],
                                                                            in_=mt[0:nseq, q4 * 128:(q4 + 1) * 128],
                                                                            identity=ident[0:nseq, 0:nseq])),
                        [mtn, "ident"], [tbn])
                c0 = ki * 8 + (blk % 2) * 4
                ADD("dve", (lambda e, c0=c0, tb_=tb_: e.tensor_copy(
                    out=modfm[:, c0:c0 + 4, :], in_=tb_[:, 0:4 * nseq].rearrange("p (c b) -> p c b", c=4))),
                    [tbn], ["modfm"])
        for b_ in range(NBLK):
            ADD("pool", (lambda e, b_=b_: e.dma_start(out=wsc_d[b_], in_=wall_d[b_])),
                ["wsc%d" % (b_ - 2)] if b_ >= 2 else [], ["wsc%d" % b_], dma=True)

        wstate = {"next": 0}
        total_blocks = nseq * NW * NBLK

        def ensure_loaded(g):
            lim = min(g + NWB, total_blocks)
            while wstate["next"] < lim:
                n = wstate["next"]
                slot = n % NWB
                blk = n % NBLK
                ADD("sp", (lambda e, slot=slot, blk=blk: e.dma_start(out=wbuf[:, slot], in_=wsc_d[blk])),
                    ["wsc%d" % blk], ["wb%d" % slot], dma=True)
                wstate["next"] += 1

        def wblk(g):
            ensure_loaded(g)
            slot = g % NWB
            return wbuf[:, slot], "wb%d" % slot

        pc = {"bank": 0}

        def xst_load(tok0, j):
            ADD("sp", (lambda e: e.dma_start(out=xst[:, j % 2, :], in_=x_d[tok0 + j * P:tok0 + (j + 1) * P, :])),
                [], ["xst%d" % (j % 2)], dma=True)

        def nbank():
            pc["bank"] += 1
            return bk(pc["bank"])

        gctr = 0
        for sq in range(nseq):
            ADD("sp", (lambda e, sq=sq: e.dma_start(out=g12b[:].rearrange("p a d -> p (a d)"),
                                                    in_=gsc_d[sq:sq + 1, :].partition_broadcast(P))),
                ["gsc"], ["g12b"], dma=True)
            ADD("pool", lambda e: e.memset(hist_l[:], 0.0), [], ["hist_l%d" % c_ for c_ in range(8)])
            ADD("pool", lambda e: e.memset(hstate[:], 0.0), [], ["hstate%d" % c_ for c_ in range(8)])
            ADD("pool", lambda e: e.memset(hist_f[:], 0.0), [], ["hist_f%d" % c_ for c_ in range(48)])
            for w in range(NW):
                tok0 = sq * S + w * T
                slot = w % 2
                pslot = 1 - slot
                if sq == 0 and w == 0:
                    xst_load(tok0, 0)
                    xst_load(tok0, 1)
                for pair in range(2):
                    if pair == 1:
                        xst_load(tok0, 2)
                        xst_load(tok0, 3)
                    for kc in range(KC):
                        pb_, pbn = nbank()
                        for t_ in range(2):
                            ADD("pe", (lambda e, kc=kc, t_=t_, pb_=pb_: e.transpose(
                                out=pb_[:, t_ * 128:(t_ + 1) * 128], in_=xst[:, t_, kc * 128:(kc + 1) * 128], identity=ident[:])),
                                ["xst%d" % t_, "ident"], [pbn])
                        ADD("act", (lambda e, kc=kc, pb_=pb_, sq=sq, pair=pair: e.activation(
                            out=uT[:, kc, pair * 256:(pair + 1) * 256], in_=pb_[:, 0:256], func=AF.Identity,
                            scale=modfm[:, 8 + kc, sq:sq + 1], bias=modfm[:, kc, sq:sq + 1])),
                            [pbn, "modfm"], ["uT%d" % kc])
                for j in range(4):
                    ADD("pool", (lambda e, tok0=tok0, j=j: e.dma_start(
                        out=x1tok[:, j, :], in_=x_d[tok0 + j * P:tok0 + (j + 1) * P, :])),
                        [], ["x1t%d" % j], dma=True)
                uTr = ["uT%d" % k for k in range(KC)]

                pg.handoff(ALL_A, M_BUFS)
                for g in (2,):
                    wb_, wbn = wblk(gctr); gctr += 1
                    if g < 2:
                        for j in range(4):
                            pb_, pbn = nbank()
                            for kc in range(KC):
                                lhs = uT[:, kc, j * 128:(j + 1) * 128] if g == 0 else uT[:, kc, j:T:4]
                                ADD("pe", (lambda e, kc=kc, lhs=lhs, pb_=pb_, wb_=wb_: e.matmul(
                                    pb_[:, :], lhsT=lhs, rhs=wb_[:, kc, :], start=(kc == 0), stop=(kc == KC - 1))),
                                    [wbn, "uT%d" % kc], [pbn])
                            dst = (V0 if g == 0 else V1)[:, slot, j, :]
                            ADD("dve", (lambda e, dst=dst, pb_=pb_: e.tensor_copy(out=dst, in_=pb_[:, :])),
                                [pbn, "V%d_all" % g], ["V%d_%d" % (g, slot)])
                    else:
                        for c4 in range(4):
                            pb_, pbn = nbank()
                            for kc in range(KC):
                                ADD("pe", (lambda e, kc=kc, c4=c4, pb_=pb_, wb_=wb_: e.matmul(
                                    pb_[:, :], lhsT=uT[:, kc, c4:T:4], rhs=wb_[:, kc, :],
                                    start=(kc == 0), stop=(kc == KC - 1))), [wbn, "uT%d" % kc], [pbn])
                            ADD("act", (lambda e, c4=c4, pb_=pb_: e.activation(out=a_v2st[:, c4, :], in_=pb_[:, :], func=AF.Copy)),
                                [pbn], ["v2st%d" % c4])
                            for sub in range(4):
                                ADD("pool", (lambda e, c4=c4, sub=sub, w=w: e.dma_start(
                                    out=V2[w * 32:(w + 1) * 32, c4 + 4 * sub, :], in_=a_v2st[sub:P:4, c4, :])),
                                    ["v2st%d" % c4, "V2_all"], ["V2"], dma=True)

                for blk in range(6):
                    wb_, wbn = wblk(gctr); gctr += 1
                    g = blk % 3
                    isq = blk < 3
                    for hp in range(HP):
                        pb_, pbn = nbank()
                        for kc in range(KC):
                            ADD("pe", (lambda e, kc=kc, hp=hp, pb_=pb_, wb_=wb_: e.matmul(
                                pb_[:, :], lhsT=wb_[:, kc, hp * 128:(hp + 1) * 128], rhs=uT[:, kc, :],
                                start=(kc == 0), stop=(kc == KC - 1))), [wbn, "uT%d" % kc], [pbn])
                        if g == 0:
                            src = pb_[:, :]
                            dst = a_qT[0][:, hp, :] if isq else K0[:, hp, slot, :]
                        elif g == 1:
                            src = pb_[:, :].rearrange("p (m c) -> p c m", c=4)
                            dst = (a_qT[1][:, hp, :] if isq else K1[:, hp, slot, :]).rearrange("p (c m) -> p c m", c=4)
                        else:
                            src = pb_[:, :].rearrange("p (m c) -> p c m", c=16)
                            dst = (a_qT[2][:, hp, :].rearrange("p (c m) -> p c m", c=16) if isq
                                   else K2[:, hp, :, w * 32:(w + 1) * 32])
                        wn = ("q%d_%d" % (g, hp)) if isq else ("K%d_%d" % (g, hp))
                        eng = "act" if isq else "dve"
                        if isq:
                            ADD("act", (lambda e, src=src, dst=dst: e.activation(out=dst, in_=src, func=AF.Copy)),
                                [pbn], [wn])
                        else:
                            ADD("dve", (lambda e, src=src, dst=dst: e.tensor_copy(out=dst, in_=src)),
                                [pbn, "K%d_all" % g], [wn])
                for g in (0, 1):
                    wb_, wbn = wblk(gctr); gctr += 1
                    if g < 2:
                        for j in range(4):
                            pb_, pbn = nbank()
                            for kc in range(KC):
                                lhs = uT[:, kc, j * 128:(j + 1) * 128] if g == 0 else uT[:, kc, j:T:4]
                                ADD("pe", (lambda e, kc=kc, lhs=lhs, pb_=pb_, wb_=wb_: e.matmul(
                                    pb_[:, :], lhsT=lhs, rhs=wb_[:, kc, :], start=(kc == 0), stop=(kc == KC - 1))),
                                    [wbn, "uT%d" % kc], [pbn])
                            dst = (V0 if g == 0 else V1)[:, slot, j, :]
                            ADD("dve", (lambda e, dst=dst, pb_=pb_: e.tensor_copy(out=dst, in_=pb_[:, :])),
                                [pbn, "V%d_all" % g], ["V%d_%d" % (g, slot)])
                    else:
                        for c4 in range(4):
                            pb_, pbn = nbank()
                            for kc in range(KC):
                                ADD("pe", (lambda e, kc=kc, c4=c4, pb_=pb_, wb_=wb_: e.matmul(
                                    pb_[:, :], lhsT=uT[:, kc, c4:T:4], rhs=wb_[:, kc, :],
                                    start=(kc == 0), stop=(kc == KC - 1))), [wbn, "uT%d" % kc], [pbn])
                            ADD("act", (lambda e, c4=c4, pb_=pb_: e.activation(out=a_v2st[:, c4, :], in_=pb_[:, :], func=AF.Copy)),
                                [pbn], ["v2st%d" % c4])
                            for sub in range(4):
                                ADD("pool", (lambda e, c4=c4, sub=sub, w=w: e.dma_start(
                                    out=V2[w * 32:(w + 1) * 32, c4 + 4 * sub, :], in_=a_v2st[sub:P:4, c4, :])),
                                    ["v2st%d" % c4, "V2_all"], ["V2"], dma=True)

                items = []
                rot = {"L": 0, "lt": 0, "pT": 0}
                NPT = 5

                def s1_common(mm_list, width, tb_ap, view, zero_prev):
                    L_, Ln = bk(rot["L"] % 4); rot["L"] += 1
                    lt = a_lt[rot["lt"] % 4]; ltn = "lt%d" % (rot["lt"] % 4); rot["lt"] += 1
                    pT = a_pT[rot["pT"] % NPT]; ptn = "pT%d" % (rot["pT"] % NPT); rot["pT"] += 1
                    for (c0, c1, lhs, rhs, rd) in mm_list:
                        ADD("pe", (lambda e, L_=L_, c0=c0, c1=c1, lhs=lhs, rhs=rhs: e.matmul(
                            L_[:, c0:c1], lhsT=lhs, rhs=rhs, start=True, stop=True, skip_group_check=True)), rd, [Ln])
                    if view is None:
                        o_ap, i_ap = lt[:, 0:width], L_[:, 0:width]
                    else:
                        o_ap = lt[:, 0:width].rearrange("p (r m) -> p r m", r=view)
                        i_ap = L_[:, 0:width].rearrange("p (r m) -> p r m", r=view)
                    ADD("dve", (lambda e, o_ap=o_ap, i_ap=i_ap, tb_ap=tb_ap: e.scalar_tensor_tensor(
                        out=o_ap, in0=i_ap, scalar=0.125, in1=tb_ap, op0=ALU.mult, op1=ALU.add)), [Ln, "tabs"], [ltn])
                    ADD("act", (lambda e, lt=lt, pT=pT, width=width: e.activation(
                        out=pT[:, 0:width], in_=lt[:, 0:width], func=AF.Exp)), [ltn], [ptn])
                    if zero_prev:
                        ADD("act", (lambda e, pT=pT: e.activation(out=pT[:, 0:128], in_=pT[:, 0:128],
                                                                  func=AF.Copy, scale=0.0)), [ptn], [ptn])
                    return pT, ptn

                for hp in range(HP):
                    X_, Xn = bk(4 + 2 * (hp % 2))
                    S_, Sn = bk(5 + 2 * (hp % 2))
                    for h2 in range(2):
                        h = hp * 2 + h2
                        rows = slice(h2 * 64, h2 * 64 + 64)
                        first = [True]

                        def pv(vt, vreads, pT_ap, ptn, ocols, h=h, rows=rows, first=first, X_=X_, S_=S_, Xn=Xn, Sn=Sn):
                            st = first[0]
                            first[0] = False
                            ADD("pe", (lambda e: e.matmul(X_[rows, ocols], lhsT=vt[:, h * 64:(h + 1) * 64], rhs=pT_ap,
                                                          start=st, stop=True, skip_group_check=True)),
                                vreads + [ptn], [Xn])
                            ADD("pe", (lambda e: e.matmul(S_[rows, ocols], lhsT=ones_bf[:, :], rhs=pT_ap,
                                                          start=st, stop=True, skip_group_check=True)),
                                ["ones", ptn], [Sn])

                        for g in range(2):
                            Kt = K0 if g == 0 else K1
                            Vt = V0 if g == 0 else V1
                            for j in range(4):
                                qap = a_qT[g][rows, hp, j * 128:(j + 1) * 128]
                                if g == 0:
                                    kprev = Kt[rows, hp, pslot, 384:512] if j == 0 else Kt[rows, hp, slot, (j - 1) * 128:j * 128]
                                    vprev = Vt[:, pslot, 3, :] if j == 0 else Vt[:, slot, j - 1, :]
                                    vpn = "V0_%d" % (pslot if j == 0 else slot)
                                    ocols = slice(j * 128, (j + 1) * 128)
                                else:
                                    kprev = Kt[rows, hp, pslot, j * 128:(j + 1) * 128]
                                    vprev = Vt[:, pslot, j, :]
                                    vpn = "V1_%d" % pslot
                                    ocols = slice(j, T, 4)
                                kcur = Kt[rows, hp, slot, j * 128:(j + 1) * 128]
                                vcur = Vt[:, slot, j, :]
                                kn = "K%d_%d" % (g, hp)
                                qn = "q%d_%d" % (g, hp)
                                tb = tabs[:, (g * 16 + h * 2):(g * 16 + h * 2 + 2), :].rearrange("p a q -> p (a q)")
                                mm = [(0, 128, kprev, qap, [kn, qn, "K%d_all" % g]), (128, 256, kcur, qap, [kn, qn])]
                                zp = (w == 0 and (g == 1 or j == 0))

                                def st1(mm=mm, tb=tb, zp=zp):
                                    return s1_common(mm, 256, tb, None, zp)

                                def st2(r_, pv=pv, vprev=vprev, vcur=vcur, vpn=vpn, g=g, ocols=ocols):
                                    pT, ptn = r_
                                    pv(vprev, [vpn, "V%d_all" % g], pT[:, 0:128], ptn, ocols)
                                    pv(vcur, ["V%d_%d" % (g, slot)], pT[:, 128:256], ptn, ocols)
                                items.append([st1, st2])
                        mm = [(r * 32, (r + 1) * 32, K2[rows, hp, r, :], a_qT[2][rows, hp, r * 32:(r + 1) * 32],
                               ["K2_%d" % hp, "q2_%d" % hp, "K2_all"]) for r in range(16)]
                        tsl = tabs[:, 32 + h, w * 32:(w + 1) * 32]
                        tb2 = bass.AP(tensor=tsl.tensor, offset=tsl.offset, ap=[list(tsl.ap[0]), [0, 16], [1, 32]])

                        def st1(mm=mm, tb2=tb2):
                            return s1_common(mm, 512, tb2, 16, False)

                        def st2(r_, pv=pv):
                            pT, ptn = r_
                            for r in range(16):
                                pv(V2[:, r, :], ["V2", "V2_all"], pT[:, r * 32:(r + 1) * 32], ptn, slice(r, T, 16))
                        items.append([st1, st2])

                    def fin(r_, prev=items[-1][1], S_=S_, X_=X_, Sn=Sn, Xn=Xn, hp=hp):
                        prev(r_)
                        ADD("dve", (lambda e: e.reciprocal(out=a_rec, in_=S_[:, :])), [Sn], ["rec"])
                        ADD("dve", (lambda e: e.tensor_tensor(out=oaT[:, hp, :], in0=X_[:, :], in1=a_rec, op=ALU.mult)),
                            [Xn, "rec"], ["oaT%d" % hp])
                    items[-1][1] = fin
                LA = 3
                nit = len(items)
                resv = [None] * nit
                for i in range(nit + LA):
                    if i < nit:
                        resv[i] = items[i][0]()
                    if i - LA >= 0:
                        items[i - LA][1](resv[i - LA])

                pg.handoff(ALL_A, L_BUFS)

                def nbank():
                    pc["bank"] += 1
                    return bk(pc["bank"])
                for cp in range(4):
                    wb_, wbn = wblk(gctr); gctr += 1
                    cs = (2 * cp, 2 * cp + 1)
                    Ab, Bb, Cb, Db = {}, {}, {}, {}
                    TS = {}
                    for c in cs:
                        cc = c % 2
                        Ab[c] = nbank()
                        Bb[c] = nbank()
                        TS[c] = (l_f32[cc * 7:(cc + 1) * 7], ["%s%d" % (n_, cc) for n_ in ("xr", "th_r", "th_i", "a_", "s_", "hh", "gel")],
                                 l_xrb[cc], "xrb%d" % cc)
                        A_, An = Ab[c]
                        B_, Bn = Bb[c]
                        for kc in range(KC):
                            ADD("pe", (lambda e, kc=kc, A_=A_, wb_=wb_, cc=cc: e.matmul(
                                A_[:, :], lhsT=wb_[:, kc, cc * 128:(cc + 1) * 128], rhs=uT[:, kc, :],
                                start=(kc == 0), stop=(kc == KC - 1))), [wbn, "uT%d" % kc], [An])
                        for kc in range(KC):
                            ADD("pe", (lambda e, kc=kc, B_=B_, wb_=wb_, cc=cc: e.matmul(
                                B_[:, :], lhsT=wb_[:, kc, (2 + cc) * 128:(3 + cc) * 128], rhs=uT[:, kc, :],
                                start=(kc == 0), stop=(kc == KC - 1))), [wbn, "uT%d" % kc], [Bn])
                    lcwf = lambda c, k: pvec[:, PV_LCW + c * 4 + k:PV_LCW + c * 4 + k + 1]
                    for c in cs:
                        (xr, th_r, th_i, a_, s_, hh, gel), (xrn, thrn, thin, an, sn, hhn, geln), xrb, xrbn = TS[c]
                        A_, An = Ab[c]
                        ADD("act", (lambda e, A_=A_, xr=xr, w3=lcwf(c, 3), bb_=pvec[:, PV_LCB + c:PV_LCB + c + 1]: e.activation(
                            out=xr, in_=A_[:, :], func=AF.Identity, scale=w3, bias=bb_)), [An, "pvec"], [xrn])
                    for c in cs:
                        (xr, th_r, th_i, a_, s_, hh, gel), (xrn, thrn, thin, an, sn, hhn, geln), xrb, xrbn = TS[c]
                        A_, An = Ab[c]
                        for k in range(3):
                            sh = 3 - k
                            ADD("dve", (lambda e, xr=xr, wk=lcwf(c, k), sh=sh, c=c: e.scalar_tensor_tensor(
                                out=xr[:, 0:sh], in0=hist_l[:, c, 3 - sh:3], scalar=wk, in1=xr[:, 0:sh],
                                op0=ALU.mult, op1=ALU.add)), ["hist_l%d" % c, "pvec", xrn], [xrn])
                        for k in range(3):
                            sh = 3 - k
                            ADD("dve", (lambda e, A_=A_, xr=xr, wk=lcwf(c, k), sh=sh: e.scalar_tensor_tensor(
                                out=xr[:, sh:T], in0=A_[:, 0:T - sh], scalar=wk, in1=xr[:, sh:T], op0=ALU.mult, op1=ALU.add)),
                                [An, "pvec", xrn], [xrn])
                    for c in cs:
                        (xr, th_r, th_i, a_, s_, hh, gel), (xrn, thrn, thin, an, sn, hhn, geln), xrb, xrbn = TS[c]
                        B_, Bn = Bb[c]
                        ADD("act", (lambda e, B_=B_, gel=gel: e.activation(out=gel, in_=B_[:, :], func=AF.Gelu_apprx_tanh)),
                            [Bn], [geln])
                    for c in cs:
                        (xr, th_r, th_i, a_, s_, hh, gel), (xrn, thrn, thin, an, sn, hhn, geln), xrb, xrbn = TS[c]
                        A_, An = Ab[c]
                        ADD("act", (lambda e, A_=A_, c=c: e.activation(out=hist_l[:, c, :], in_=A_[:, T - 3:T], func=AF.Copy)),
                            [An], ["hist_l%d" % c])
                        ADD("pool", (lambda e, xr=xr, xrb=xrb: e.tensor_copy(out=xrb, in_=xr)), [xrn], [xrbn])
                    for c in cs:
                        (xr, th_r, th_i, a_, s_, hh, gel), (xrn, thrn, thin, an, sn, hhn, geln), xrb, xrbn = TS[c]
                        Cb[c] = nbank()
                        Db[c] = nbank()
                        C_, Cn = Cb[c]
                        D_, Dn = Db[c]
                        ADD("pe", (lambda e, C_=C_, c=c, xrb=xrb: e.matmul(C_[:, :], lhsT=lruw[:, c, 0, :], rhs=xrb, start=True, stop=True)),
                            ["lruw", xrbn], [Cn])
                        ADD("pe", (lambda e, D_=D_, c=c, xrb=xrb: e.matmul(D_[:, :], lhsT=lruw[:, c, 1, :], rhs=xrb, start=True, stop=True)),
                            ["lruw", xrbn], [Dn])
                    for c in cs:
                        (xr, th_r, th_i, a_, s_, hh, gel), (xrn, thrn, thin, an, sn, hhn, geln), xrb, xrbn = TS[c]
                        C_, Cn = Cb[c]
                        D_, Dn = Db[c]
                        ADD("act", (lambda e, C_=C_, c=c, th_r=th_r: e.activation(out=th_r, in_=C_[:, :], func=AF.Tanh, scale=0.5,
                                                                                 bias=lrup[:, 2, c:c + 1])), [Cn, "lrup"], [thrn])
                        ADD("act", (lambda e, D_=D_, c=c, th_i=th_i: e.activation(out=th_i, in_=D_[:, :], func=AF.Tanh, scale=0.5,
                                                                                 bias=lrup[:, 3, c:c + 1])), [Dn, "lrup"], [thin])
                    for c in cs:
                        (xr, th_r, th_i, a_, s_, hh, gel), (xrn, thrn, thin, an, sn, hhn, geln), xrb, xrbn = TS[c]
                        ADD("act", (lambda e, c=c, a_=a_, th_r=th_r: e.activation(out=a_, in_=th_r, func=AF.Exp, scale=lrup[:, 0, c:c + 1],
                                                                                 bias=lrup[:, 0, c:c + 1])), [thrn, "lrup"], [an])
                        ADD("act", (lambda e, c=c, s_=s_, th_r=th_r: e.activation(out=s_, in_=th_r, func=AF.Exp, scale=lrup[:, 1, c:c + 1],
                                                                                 bias=lrup[:, 1, c:c + 1])), [thrn, "lrup"], [sn])
                    for c in cs:
                        (xr, th_r, th_i, a_, s_, hh, gel), (xrn, thrn, thin, an, sn, hhn, geln), xrb, xrbn = TS[c]
                        ADD("act", (lambda e, s_=s_: e.activation(out=s_, in_=s_, func=AF.Sqrt, scale=-1.0, bias=1.0)), [sn], [sn])
                    for c in cs:
                        (xr, th_r, th_i, a_, s_, hh, gel), (xrn, thrn, thin, an, sn, hhn, geln), xrb, xrbn = TS[c]
                        ADD("dve", (lambda e, th_i=th_i, xr=xr: e.scalar_tensor_tensor(
                            out=th_i, in0=th_i, scalar=1.0, in1=xr, op0=ALU.add, op1=ALU.mult)), [thin, xrn], [thin])
                        ADD("dve", (lambda e, th_i=th_i, s_=s_: e.scalar_tensor_tensor(
                            out=th_i, in0=th_i, scalar=0.5, in1=s_, op0=ALU.mult, op1=ALU.mult)), [thin, sn], [thin])
                        ADD("dve", (lambda e, c=c, hh=hh, a_=a_, th_i=th_i: e.tensor_tensor_scan(
                            out=hh, data0=a_, data1=th_i, initial=hstate[:, c:c + 1], op0=ALU.mult, op1=ALU.add)),
                            [an, thin, "hstate%d" % c], [hhn])
                        ADD("dve", (lambda e, c=c, hh=hh: e.tensor_copy(out=hstate[:, c:c + 1], in_=hh[:, T - 1:T])),
                            [hhn], ["hstate%d" % c])
                        ADD("dve", (lambda e, c=c, hh=hh, gel=gel: e.tensor_tensor(out=l_hgT[:, c, :], in0=hh, in1=gel, op=ALU.mult)),
                            [hhn, geln], ["hgT%d" % c])

                pg.handoff(ALL_A, E_BUFS)
                for c in range(KC):
                    wb_, wbn = wblk(gctr); gctr += 1
                    GA_, GAn = nbank()
                    GR_, GRn = nbank()
                    YA_, YAn = nbank()
                    YL_, YLn = nbank()
                    for kc in range(KC):
                        ADD("pe", (lambda e, kc=kc, GA_=GA_, wb_=wb_: e.matmul(
                            GA_[:, :], lhsT=wb_[:, kc, 0:128], rhs=uT[:, kc, :], start=(kc == 0), stop=(kc == KC - 1))),
                            [wbn, "uT%d" % kc], [GAn])
                    for kc in range(KC):
                        ADD("pe", (lambda e, kc=kc, GR_=GR_, wb_=wb_: e.matmul(
                            GR_[:, :], lhsT=wb_[:, kc, 128:256], rhs=uT[:, kc, :], start=(kc == 0), stop=(kc == KC - 1))),
                            [wbn, "uT%d" % kc], [GRn])
                    for kc in range(KC):
                        ADD("pe", (lambda e, kc=kc, YL_=YL_, wb_=wb_: e.matmul(
                            YL_[:, :], lhsT=wb_[:, kc, 256:384], rhs=l_hgT[:, kc, :], start=(kc == 0), stop=(kc == KC - 1))),
                            [wbn, "hgT%d" % kc], [YLn])
                    for kc in range(4):
                        ADD("pe", (lambda e, kc=kc, YA_=YA_, wb_=wb_: e.matmul(
                            YA_[:, :], lhsT=wb_[:, kc, 384:512], rhs=oaT[:, kc, :], start=(kc == 0), stop=(kc == 3))),
                            [wbn, "oaT%d" % kc], [YAn])
                    sA, sR, m1 = m_sig
                    ADD("act", (lambda e, GA_=GA_: e.activation(out=sA, in_=GA_[:, :], func=AF.Sigmoid)), [GAn], ["sA"])
                    ADD("act", (lambda e, GR_=GR_: e.activation(out=sR, in_=GR_[:, :], func=AF.Sigmoid)), [GRn], ["sR"])
                    ADD("dve", (lambda e, YA_=YA_: e.tensor_tensor(out=sA, in0=YA_[:, :], in1=sA, op=ALU.mult)), [YAn, "sA"], ["sA"])
                    ADD("dve", (lambda e, YL_=YL_: e.tensor_tensor(out=sR, in0=YL_[:, :], in1=sR, op=ALU.mult)), [YLn, "sR"], ["sR"])
                    ADD("dve", (lambda e, c=c: e.tensor_tensor(out=m_mT[:, c, :], in0=sA, in1=sR, op=ALU.add)),
                        ["sA", "sR"], ["mT%d" % c])

                def layernorm(j, gi):
                    for hf in range(2):
                        ADD("dve", (lambda e, hf=hf: e.bn_stats(out=stats[:, hf, :], in_=x1tok[:, j, hf * 512:(hf + 1) * 512])),
                            ["x1t%d" % j], ["stats"])
                    ADD("dve", lambda e: e.bn_aggr(out=mv[:, 0:2], in_=stats[:].rearrange("p a b -> p (a b)")), ["stats"], ["mv"])
                    ADD("act", lambda e: e.activation(out=mv[:, 2:3], in_=mv[:, 1:2], func=AF.Sqrt, bias=LN_EPS), ["mv"], ["mv"])
                    ADD("dve", lambda e: e.reciprocal(out=mv[:, 3:4], in_=mv[:, 2:3]), ["mv"], ["mv"])
                    ADD("dve", lambda e: e.scalar_tensor_tensor(out=x1tok[:, j, :], in0=x1tok[:, j, :], scalar=mv[:, 0:1],
                                                                in1=lnbc[:, gi, :], op0=ALU.subtract, op1=ALU.mult),
                        ["x1t%d" % j, "mv", "lnbc"], ["x1t%d" % j])
                    ADD("dve", lambda e: e.scalar_tensor_tensor(out=x1tok[:, j, :], in0=x1tok[:, j, :], scalar=mv[:, 3:4],
                                                                in1=lnbc[:, gi + 1, :], op0=ALU.mult, op1=ALU.add),
                        ["x1t%d" % j, "mv", "lnbc"], ["x1t%d" % j])

                wbs = [wblk(gctr)]
                wbs.append((wbuf[:, (gctr + 1) % NWB], "wb%d" % ((gctr + 1) % NWB)))
                gctr += 2
                for j in range(4):
                    for nb in range(2):
                        wb_, wbn = wbs[nb]
                        pb_, pbn = nbank()
                        for kc in range(KC):
                            ADD("pe", (lambda e, kc=kc, j=j, pb_=pb_, wb_=wb_: e.matmul(
                                pb_[:, :], lhsT=m_mT[:, kc, j * 128:(j + 1) * 128], rhs=wb_[:, kc, :],
                                start=(kc == 0), stop=(kc == KC - 1))), [wbn, "mT%d" % kc], [pbn])
                        zt = ztmp[:, nb, :]
                        ztn = "zt%d" % nb
                        ADD("dve", (lambda e, pb_=pb_, zt=zt, nb=nb: e.tensor_tensor(
                            out=zt, in0=pb_[:, :], in1=g12b[:, 0, nb * 512:(nb + 1) * 512], op=ALU.mult)),
                            [pbn, "g12b"], [ztn])
                        xs = x1tok[:, j, nb * 512:(nb + 1) * 512]
                        ADD("dve", (lambda e, xs=xs, zt=zt: e.scalar_tensor_tensor(
                            out=xs, in0=xs, scalar=ALPHA, in1=zt, op0=ALU.mult, op1=ALU.add)), [ztn, "x1t%d" % j], ["x1t%d" % j])
                    layernorm(j, 0)
                for pair in range(2):
                    for kc in range(KC):
                        pb_, pbn = nbank()
                        for t_ in range(2):
                            j = pair * 2 + t_
                            ADD("pe", (lambda e, kc=kc, j=j, t_=t_, pb_=pb_: e.transpose(
                                out=pb_[:, t_ * 128:(t_ + 1) * 128], in_=x1tok[:, j, kc * 128:(kc + 1) * 128], identity=ident[:])),
                                ["x1t%d" % j, "ident"], [pbn])
                        ADD("act", (lambda e, kc=kc, pb_=pb_, sq=sq, pair=pair: e.activation(
                            out=uT[:, kc, pair * 256:(pair + 1) * 256], in_=pb_[:, 0:256], func=AF.Identity,
                            scale=modfm[:, 24 + kc, sq:sq + 1], bias=modfm[:, 16 + kc, sq:sq + 1])),
                            [pbn, "modfm"], ["uT%d" % kc])

                pg.handoff(ALL_A, F_BUFS)
                if not (sq == nseq - 1 and w == NW - 1):
                    xst_load(tok0 + T, 0)
                    xst_load(tok0 + T, 1)
                hfn = ["hist_f%d" % c_ for c_ in range(48)]
                Wk = lambda k: pvec[:, PV_FCW + k:PV_FCW + 144:3]
                ADD("dve", lambda e: e.tensor_tensor(out=corr[:, :, 1], in0=Wk(0), in1=hist_f[:, :, 1], op=ALU.mult),
                    hfn + ["pvec"], ["corr"])
                ADD("dve", lambda e: e.tensor_tensor(out=corr[:, :, 0], in0=Wk(0), in1=hist_f[:, :, 0], op=ALU.mult),
                    hfn + ["pvec"], ["corr"])
                ADD("dve", lambda e: e.tensor_tensor(out=ctmp[:, :], in0=Wk(1), in1=hist_f[:, :, 1], op=ALU.mult),
                    hfn + ["pvec"], ["ctmp"])
                ADD("dve", lambda e: e.tensor_tensor(out=corr[:, :, 0], in0=corr[:, :, 0], in1=ctmp[:, :], op=ALU.add),
                    ["corr", "ctmp"], ["corr"])
                fstate = {}

                def ffn_a(pr):
                    if pr % 2 == 0:
                        fstate["wb"] = wblk(fstate["g"]); fstate["g"] += 1
                    wb_, wbn = fstate["wb"]
                    pc2 = pr % 2
                    A_, An = nbank()
                    B_, Bn = nbank()
                    for kc in range(KC):
                        ADD("pe", (lambda e, kc=kc: e.matmul(
                            A_[:, :], lhsT=wb_[:, kc, pc2 * 128:(pc2 + 1) * 128], rhs=uT[:, kc, :],
                            start=(kc == 0), stop=(kc == KC - 1))), [wbn, "uT%d" % kc], [An])
                    for kc in range(KC):
                        ADD("pe", (lambda e, kc=kc: e.matmul(
                            B_[:, :], lhsT=wb_[:, kc, (2 + pc2) * 128:(3 + pc2) * 128], rhs=uT[:, kc, :],
                            start=(kc == 0), stop=(kc == KC - 1))), [wbn, "uT%d" % kc], [Bn])
                    fs = pr % 2
                    cv, cg, gl = f_tmp[3 * fs:3 * fs + 3]
                    cvn, cgn, gln = "cv%d" % fs, "cg%d" % fs, "gl%d" % fs
                    halves = ((A_, An, pr, cv, cvn), (B_, Bn, 24 + pr, cg, cgn))
                    for (PB, PBn, ch, dst, dn) in halves:
                        ADD("act", (lambda e, PB=PB, ch=ch, dst=dst: e.activation(
                            out=dst, in_=PB[:, :], func=AF.Identity, scale=pvec[:, PV_FCW + ch * 3 + 2:PV_FCW + ch * 3 + 3],
                            bias=pvec[:, PV_FCB + ch:PV_FCB + ch + 1])), [PBn, "pvec"], [dn])
                    for (PB, PBn, ch, dst, dn) in halves:
                        ADD("dve", (lambda e, ch=ch, dst=dst: e.tensor_tensor(
                            out=dst[:, 0:2], in0=dst[:, 0:2], in1=corr[:, ch, :], op=ALU.add)), ["corr", dn], [dn])
                    for (PB, PBn, ch, dst, dn) in halves:
                        for k in range(2):
                            sh = 2 - k
                            ADD("dve", (lambda e, PB=PB, dst=dst, ch=ch, k=k, sh=sh: e.scalar_tensor_tensor(
                                out=dst[:, sh:T], in0=PB[:, 0:T - sh], scalar=pvec[:, PV_FCW + ch * 3 + k:PV_FCW + ch * 3 + k + 1],
                                in1=dst[:, sh:T], op0=ALU.mult, op1=ALU.add)), [PBn, "pvec", dn], [dn])
                    return (halves, cv, cg, gl, cvn, cgn, gln, pr)

                def ffn_b(st):
                    halves, cv, cg, gl, cvn, cgn, gln, pr = st
                    for (PB, PBn, ch, dst, dn) in halves:
                        ADD("act", (lambda e, PB=PB, ch=ch: e.activation(out=hist_f[:, ch, :], in_=PB[:, T - 2:T], func=AF.Copy)),
                            [PBn, "corr"], ["hist_f%d" % ch])
                    ADD("act", (lambda e: e.activation(out=gl, in_=cg, func=AF.Gelu_apprx_tanh)), [cgn], [gln])
                    ADD("pool", (lambda e: e.tensor_tensor(out=f_gT[:, pr, :], in0=gl, in1=cv, op=ALU.mult)),
                        [gln, cvn], ["gT%d" % pr])

                fstate["g"] = gctr
                fprev = None
                for pr in range(25):
                    cur = ffn_a(pr) if pr < 24 else None
                    if fprev is not None:
                        ffn_b(fprev)
                    fprev = cur
                gctr = fstate["g"]

                for nb in range(2):
                    accs = [nbank() for _ in range(4)]
                    for kg in range(3):
                        wb_, wbn = wblk(gctr); gctr += 1
                        for j in range(4):
                            pb_, pbn = accs[j]
                            for kc in range(KC):
                                ADD("pe", (lambda e, kc=kc, j=j, kg=kg, pb_=pb_, wb_=wb_: e.matmul(
                                    pb_[:, :], lhsT=f_gT[:, kg * 8 + kc, j * 128:(j + 1) * 128], rhs=wb_[:, kc, :],
                                    start=(kg == 0 and kc == 0), stop=(kg == 2 and kc == KC - 1))), [wbn, "gT%d" % (kg * 8 + kc)], [pbn])
                    for j in range(4):
                        pb_, pbn = accs[j]
                        zt = ztmp[:, j % 2, :]
                        ztn = "zt%d" % (j % 2)
                        ADD("dve", (lambda e, pb_=pb_, zt=zt, nb=nb: e.tensor_tensor(
                            out=zt, in0=pb_[:, :], in1=g12b[:, 1, nb * 512:(nb + 1) * 512], op=ALU.mult)),
                            [pbn, "g12b"], [ztn])
                        xs = x1tok[:, j, nb * 512:(nb + 1) * 512]
                        ADD("dve", (lambda e, xs=xs, zt=zt: e.scalar_tensor_tensor(
                            out=xs, in0=xs, scalar=ALPHA, in1=zt, op0=ALU.mult, op1=ALU.add)), [ztn, "x1t%d" % j], ["x1t%d" % j])
                for j in range(4):
                    layernorm(j, 2)
                    ADD("pool", (lambda e, tok0=tok0, j=j: e.dma_start(
                        out=y_d[tok0 + j * P:tok0 + (j + 1) * P, :], in_=x1tok[:, j, :])),
                        ["x1t%d" % j], ["ystore"], dma=True)
        last = pg.add("sp", None, ["ystore"], [])
        last.deps = sorted(set(last.deps) | {o.idx for o in pg.dsem_last.values()})
        pg.emit(nc, es)
    return nc


def _rel_bucket(dist):
    dist = dist.astype(np.int32)
    nf = np.maximum(dist, 1).astype(np.float32)
    large = 16 + (np.log(nf / np.float32(16)) / np.float32(math.log(2048 / 16)) * np.float32(16)).astype(np.int32)
    large = np.minimum(large, 31)
    return np.where(dist < 16, dist, large)


def _blk(w):
    return np.ascontiguousarray(w.reshape(KC, P, w.shape[1]).transpose(1, 0, 2))


def prepare_shared(inp):
    f = lambda k: np.asarray(inp[k], dtype=np.float32)
    w_in = f("w_in")[0]
    wall = np.zeros((NBLK, P, KC, 512), np.float32)
    for pos, i in enumerate((8, 0, 1, 2, 3, 4, 5, 6, 7)):
        wall[pos] = _blk(w_in[:, 512 * i:512 * (i + 1)])
    XL, LG, GA, GR = 4608, 5632, 6656, 7680
    for i in range(4):
        cols = np.concatenate([w_in[:, XL + 256 * i:XL + 256 * (i + 1)], w_in[:, LG + 256 * i:LG + 256 * (i + 1)]], 1)
        wall[9 + i] = _blk(cols)
    wpa = f("w_proj_attn")[0]
    wpl = f("w_proj_lru")[0]
    for c in range(8):
        pa = np.zeros((1024, 128), np.float32)
        pa[0:512] = wpa[:, 128 * c:128 * (c + 1)]
        cols = np.concatenate([w_in[:, GA + 128 * c:GA + 128 * (c + 1)], w_in[:, GR + 128 * c:GR + 128 * (c + 1)],
                               wpl[:, 128 * c:128 * (c + 1)], pa], 1)
        wall[13 + c] = _blk(cols)
    wo = f("w_out")[0]
    for nb in range(2):
        wall[21 + nb] = _blk(wo[:, 512 * nb:512 * (nb + 1)])
    wu = f("ffn_w_up")[0]
    for i in range(12):
        cols = np.concatenate([wu[:, 256 * i:256 * (i + 1)], wu[:, 3072 + 256 * i:3072 + 256 * (i + 1)]], 1)
        wall[23 + i] = _blk(cols)
    wd = f("ffn_w_down")[0]
    for nb in range(2):
        for kg in range(3):
            wall[35 + nb * 3 + kg] = _blk(wd[kg * 1024:(kg + 1) * 1024, 512 * nb:512 * (nb + 1)])
    wada = f("w_ada")[0]
    wada_b = np.stack([_blk(wada[:, 512 * i:512 * (i + 1)]) for i in range(12)])
    bada = f("b_ada").reshape(1, 6144)
    rb = f("rel_bias")
    tabs = np.zeros((P, 40, 128), np.float32)
    k = np.arange(128)[:, None]
    q = np.arange(128)[None, :]
    for g, dil in enumerate((1, 4, 16)):
        dprev = q + 128 - k
        dcur = q - k
        bprev = _rel_bucket(np.maximum(dprev, 0) * dil)
        bcur = _rel_bucket(np.maximum(dcur, 0) * dil)
        for h in range(8):
            col = rb[:, g * 8 + h]
            tp = np.where(dprev <= 128, col[bprev], np.float32(NEG))
            tc = np.where(dcur >= 0, col[bcur], np.float32(NEG))
            if g < 2:
                tabs[:, g * 16 + h * 2] = tp
                tabs[:, g * 16 + h * 2 + 1] = tc
            else:
                tabs[:, 32 + h] = tc
    pvec = np.zeros((P, PV_N), np.float32)
    fm = lambda v: np.ascontiguousarray(v.reshape(-1, P).T)
    lcw = f("lru_conv_w")[0]
    pvec[:, PV_LCW:PV_LCW + 32] = np.stack([fm(lcw[kk]) for kk in range(4)], 2).reshape(P, 32)
    pvec[:, PV_LCB:PV_LCB + 8] = fm(f("lru_conv_b")[0])
    pvec[:, PV_BA:PV_BA + 8] = fm(f("lru_ba")[0])
    pvec[:, PV_BX:PV_BX + 8] = fm(f("lru_bx")[0])
    pvec[:, PV_LAM:PV_LAM + 8] = fm(f("lru_lambda")[0])
    fcw = f("ffn_conv_w")[0]
    pvec[:, PV_FCW:PV_FCW + 144] = np.stack([fm(fcw[kk]) for kk in range(3)], 2).reshape(P, 144)
    pvec[:, PV_FCB:PV_FCB + 48] = fm(f("ffn_conv_b")[0])
    lruw = np.zeros((P, 8, 2, 128), np.float32)
    for gi, nm in enumerate(("lru_wa", "lru_wx")):
        wg = f(nm)[0]
        for n in range(16):
            c, o = n // 2, (n % 2) * 64
            lruw[o:o + 64, c, gi, o:o + 64] = wg[n]
    lnp = np.stack([f("ln1_g")[0], f("ln1_b")[0], f("ln2_g")[0], f("ln2_b")[0]]).reshape(1, 4 * D)
    return {"wada": wada_b, "bada": bada, "wall": wall, "tabs": tabs.reshape(P, 40 * 128), "pvec": pvec,
            "lruw": lruw.reshape(P, 8 * 2 * 128), "lnp": lnp}


def core_inputs(inp, shared, b0, nseq):
    x = np.asarray(inp["x"], dtype=np.float32)[b0:b0 + nseq].reshape(nseq * S, D)
    c = np.asarray(inp["c"], dtype=np.float32)[b0:b0 + nseq]
    cT = np.ascontiguousarray(c.reshape(nseq, KC, P).transpose(2, 1, 0))
    m = dict(shared)
    m["x"] = np.ascontiguousarray(x)
    m["cT"] = cT
    return m


_NC_CACHE = {}


def kernel(**inputs):
    nseq = SEQ_PER_CORE
    if nseq not in _NC_CACHE:
        _NC_CACHE[nseq] = build_program(nseq)
    nc = _NC_CACHE[nseq]
    shared = prepare_shared(inputs)
    in_maps = [core_inputs(inputs, shared, i * nseq, nseq) for i in range(N_CORES)]
    res = run_bass_kernel_spmd(nc, in_maps, core_ids=list(range(N_CORES)))
    outs = [np.asarray(r["y"]).reshape(nseq, S, D) for r in res.results]
    return np.concatenate(outs, axis=0).astype(np.float32)
```

```python
import math
import numpy as np
from contextlib import ExitStack
import concourse.bass as bass
import concourse.mybir as mybir
from concourse.bass_utils import run_bass_kernel_spmd

F32 = mybir.dt.float32
BF16 = mybir.dt.bfloat16
AF = mybir.ActivationFunctionType
ALU = mybir.AluOpType

P = 128
D = 1024
KC = 8
S = 2048
T = 512
NW = S // T
H = 8
HP = 4
NBLK = 41
NWB = 3
ALPHA = 2.0 ** 0.25
LN_EPS = 1e-5
NEG = -30000.0
NDS = 28
N_CORES = 8
SEQ_PER_CORE = 4

PV_LCW = 0
PV_LCB = 32
PV_BA = 40
PV_BX = 48
PV_LAM = 56
PV_FCW = 64
PV_FCB = 208
PV_N = 256


class Buf:
    __slots__ = ("name", "w", "rs", "excl", "wreal")

    def __init__(self, name):
        self.name = name
        self.w = None
        self.rs = []
        self.excl = name.startswith("bank")
        self.wreal = None


class Op:
    __slots__ = ("idx", "eng", "fn", "deps", "signal", "semval", "dma", "dsem", "dval")


COMPUTE = ("pe", "act", "dve", "pool")


class Prog:
    def __init__(self):
        self.ops = []
        self.bufs = {}
        self.dma_cnt = {}
        self.dsem_last = {}

    def B(self, name):
        b = self.bufs.get(name)
        if b is None:
            b = self.bufs[name] = Buf(name)
        return b

    def add(self, eng, fn, reads=(), writes=(), dma=False):
        op = Op()
        op.idx = len(self.ops)
        op.eng = eng
        op.fn = fn
        op.dma = dma
        op.signal = False
        op.semval = 0
        op.dsem = -1
        op.dval = 0
        deps = {}
        rb = [self.B(r) if isinstance(r, str) else r for r in reads]
        wb = [self.B(w) if isinstance(w, str) else w for w in writes]
        for b in rb:
            if b.excl:
                if b.wreal is not None:
                    deps[b.wreal] = True
                if b.w is not None:
                    deps.setdefault(b.w, False)
                for r in b.rs:
                    deps.setdefault(r, False)
            elif b.w is not None:
                deps[b.w] = True
        for b in wb:
            if b.w is not None:
                deps.setdefault(b.w, False)
            for r in b.rs:
                deps.setdefault(r, False)
        if dma:
            base, n_ = {"sp": (0, 16), "pool": (16, 12)}[eng]
            k_ = self.dma_cnt.get(eng, 0)
            self.dma_cnt[eng] = k_ + 1
            s = base + k_ % n_
            prev = self.dsem_last.get(s)
            if prev is not None:
                deps.setdefault(prev.idx, False)
                op.dval = prev.dval + 16
            else:
                op.dval = 16
            op.dsem = s
            self.dsem_last[s] = op
        final = []
        for di, raw in deps.items():
            if di == op.idx:
                continue
            dop = self.ops[di]
            if dop.dma:
                final.append(di)
            elif dma:
                final.append(di)
                dop.signal = True
            elif dop.eng == eng:
                if eng != "pe" and raw:
                    final.append(di)
                    dop.signal = True
            else:
                final.append(di)
                dop.signal = True
        op.deps = sorted(final)
        for b in rb:
            if b.excl:
                b.w = op.idx
                b.rs = []
            else:
                b.rs.append(op.idx)
        for b in wb:
            b.w = op.idx
            b.wreal = op.idx
            b.rs = []
        self.ops.append(op)
        return op

    def handoff(self, arena_names, new_names):
        acc = set()
        for n in arena_names:
            b = self.B(n)
            if b.w is not None:
                acc.add(b.w)
            acc.update(b.rs)
        for n in new_names:
            b = self.B(n)
            b.rs = sorted(acc | set(b.rs))

    def check_deadlock(self):
        queues = {}
        for op in self.ops:
            queues.setdefault(op.eng, []).append(op)
        pos = {k: 0 for k in queues}
        done = set()
        progress = True
        while progress:
            progress = False
            for k, q in queues.items():
                while pos[k] < len(q):
                    op = q[pos[k]]
                    if all(d in done for d in op.deps):
                        done.add(op.idx)
                        pos[k] += 1
                        progress = True
                    else:
                        break
        stuck = {k: (q[pos[k]].idx, [d for d in q[pos[k]].deps if d not in done]) for k, q in queues.items() if pos[k] < len(q)}
        if stuck:
            raise RuntimeError("deadlock in program order: %r" % (stuck,))

    def emit(self, nc, es):
        self.check_deadlock()
        engs = {"pe": nc.tensor, "act": nc.scalar, "dve": nc.vector, "pool": nc.gpsimd, "sp": nc.sync}
        sems = {k: es.enter_context(nc.semaphore("s_" + k)) for k in COMPUTE}
        dsems = [es.enter_context(nc.semaphore("d_%d" % i)) for i in range(NDS)]
        cnt = {k: 0 for k in COMPUTE}
        for op in self.ops:
            if not op.dma and op.signal:
                cnt[op.eng] += 1
                op.semval = cnt[op.eng]
        known = {k: {} for k in engs}
        snap = {}
        for op in self.ops:
            e = engs[op.eng]
            kn = known[op.eng]
            need = {}
            for di in op.deps:
                dop = self.ops[di]
                if dop.dma:
                    key = ("d", dop.dsem)
                    val = dop.dval
                    sem = dsems[dop.dsem]
                else:
                    key = dop.eng
                    val = dop.semval
                    sem = sems[dop.eng]
                if kn.get(key, 0) >= val:
                    continue
                cur = need.get(key)
                if cur is None or cur[0] < val:
                    need[key] = (val, sem, di)
            for key, (val, sem, di) in need.items():
                if kn.get(key, 0) >= val:
                    continue
                e.wait_ge(sem, val)
                kn[key] = val
                sn = snap.get(di)
                if sn:
                    for k2, v2 in sn.items():
                        if kn.get(k2, 0) < v2:
                            kn[k2] = v2
            ins = op.fn(e) if op.fn is not None else None
            if op.dma:
                ins.then_inc(dsems[op.dsem], 16)
                snap[op.idx] = dict(kn)
            elif op.signal:
                ins.then_inc(sems[op.eng], 1)
                sn = dict(kn)
                snap[op.idx] = sn


def build_program(nseq):
    nc = bass.Bass("TRN2", target_bir_lowering=False)
    NTOK = nseq * S

    def din(name, shape, dt=F32):
        return nc.dram_tensor(name, list(shape), dt, kind="ExternalInput").ap()

    x_d = din("x", [NTOK, D])
    cT_d = din("cT", [P, KC, nseq])
    wada_d = din("wada", [12, P, KC, 512])
    bada_d = din("bada", [1, 6144])
    wall_d = din("wall", [NBLK, P, KC, 512])
    tabs_d = din("tabs", [P, 40 * 128])
    pvec_d = din("pvec", [P, PV_N])
    lruw_d = din("lruw", [P, 8 * 2 * 128])
    lnp_d = din("lnp", [1, 4 * D])
    y_d = nc.dram_tensor("y", [NTOK, D], F32, kind="ExternalOutput").ap()
    wsc_d = nc.dram_tensor("wsc", [NBLK, P, KC, 512], BF16, kind="Internal").ap()
    gsc_d = nc.dram_tensor("gsc", [max(nseq, 2), 2048], F32, kind="Internal").ap()

    pg = Prog()
    es = ExitStack()
    with es:
        def sb(name, shape, dt=F32):
            return es.enter_context(nc.sbuf_tensor("sb_" + name, list(shape), dt))

        ident = sb("ident", [P, P])
        ones_bf = sb("ones_bf", [P, 64], BF16)
        pvec = sb("pvec", [P, PV_N])
        lruw = sb("lruw", [P, 8, 2, 128], BF16)
        tabs = sb("tabs", [P, 40, 128], BF16)
        lnbc = sb("lnbc", [P, 4, D])
        g12b = sb("g12b", [P, 2, D])
        modfm = sb("modfm", [P, 32, nseq])
        lrup = sb("lrup", [P, 4, 8])
        hist_l = sb("hist_l", [P, 8, 3])
        hstate = sb("hstate", [P, 8])
        hist_f = sb("hist_f", [P, 48, 2])
        K0 = sb("K0", [P, HP, 2, T], BF16)
        K1 = sb("K1", [P, HP, 2, T], BF16)
        K2 = sb("K2", [P, HP, 16, 128], BF16)
        V0 = sb("V0", [P, 2, 4, 512], BF16)
        V1 = sb("V1", [P, 2, 4, 512], BF16)
        V2 = sb("V2", [P, 16, 512], BF16)
        x1tok = sb("x1tok", [P, 4, D])
        uT = sb("uT", [P, KC, T], BF16)
        wbuf = sb("wbuf", [P, NWB, KC, 512], BF16)
        oaT = sb("oaT", [P, HP, T], BF16)
        stats = sb("stats", [P, 2, 6])
        mv = sb("mv", [P, 4])
        ztmp = sb("ztmp", [P, 2, 512])
        corr = sb("corr", [P, 48, 2])
        xst = sb("xst", [P, 2, D])
        ctmp = sb("ctmp", [P, 48])
        ARENA = 19456
        arena = sb("arena", [P, ARENA], BF16)
        banks = [es.enter_context(nc.psum_tensor("bank%d" % i, [P, 512], F32)) for i in range(8)]

        def bk(i):
            return banks[i % 8], "bank%d" % (i % 8)

        def av(off, n, dt=BF16, shape=None):
            v = arena[:, off:off + n]
            if dt == F32:
                v = v.bitcast(F32)
            return v

        a_qT = [av(g * 2048, 2048).rearrange("p (h t) -> p h t", h=HP) for g in range(3)]
        a_pT = [av(6144 + i * 512, 512) for i in range(4)] + [av(13312, 512)]
        a_lt = [av(8192 + i * 1024, 1024, F32) for i in range(2)]
        a_rec = av(10240, 1024, F32)
        a_v2st = av(11264, 2048).rearrange("p (c n) -> p c n", c=4)
        l_f32 = [av(i * 1024, 1024, F32) for i in range(14)]
        l_xrb = [av(14336 + i * 512, 512) for i in range(2)]
        l_hgT = av(15360, 4096).rearrange("p (c t) -> p c t", c=KC)
        m_sig = [av(4096 + i * 1024, 1024, F32) for i in range(3)]
        m_mT = av(0, 4096).rearrange("p (c t) -> p c t", c=KC)
        f_gT = av(0, 12288).rearrange("p (c t) -> p c t", c=24)
        f_tmp = [av(12288 + i * 1024, 1024, F32) for i in range(6)]
        s_stage = av(0, 8192, F32).rearrange("p (k n) -> p k n", k=KC)
        M_BUFS = ["q%d_%d" % (g, hp) for g in range(3) for hp in range(HP)] + ["pT0", "pT1", "pT2", "pT3", "pT4", "lt0", "lt1", "rec", "v2st0", "v2st1", "v2st2", "v2st3"]
        L_BUFS = ["%s%d" % (n_, i_) for n_ in ("xr", "th_r", "th_i", "a_", "s_", "hh", "gel", "xrb") for i_ in range(2)] + ["hgT%d" % i_ for i_ in range(8)]
        E_BUFS = ["sA", "sR"] + ["mT%d" % i_ for i_ in range(8)]
        F_BUFS = ["gT%d" % i_ for i_ in range(24)] + ["%s%d" % (n_, i_) for n_ in ("cv", "cg", "gl") for i_ in range(2)]
        S_BUFS = ["sstage0", "sstage1", "mt0", "mt1"]
        ALL_A = M_BUFS + L_BUFS + E_BUFS + F_BUFS + S_BUFS

        ADD = pg.add

        ADD("sp", lambda e: e.dma_start(out=pvec[:], in_=pvec_d), [], ["pvec"], dma=True)
        ADD("sp", lambda e: e.dma_start(out=lnbc[:].rearrange("p a d -> p (a d)"),
                                        in_=lnp_d.partition_broadcast(P)),
            [], ["lnbc"], dma=True)
        ADD("pool", lambda e: e.dma_start(out=tabs[:].rearrange("p a q -> p (a q)"), in_=tabs_d), [], ["tabs"], dma=True)
        ADD("pool", lambda e: e.dma_start(out=lruw[:].rearrange("p a b c -> p (a b c)"), in_=lruw_d), [], ["lruw"], dma=True)
        ADD("pool", lambda e: e.memset(ident[:], 0.0), [], ["ident"])
        ADD("pool", lambda e: e.affine_select(out=ident[:], in_=ident[:], compare_op=ALU.not_equal, fill=1.0,
                                              base=0, pattern=[[-1, P]], channel_multiplier=1), ["ident"], ["ident"])
        ADD("pool", lambda e: e.memset(ones_bf[:], 1.0), [], ["ones"])
        for t_, nm in ((K0, "K0"), (K1, "K1"), (K2, "K2"), (V0, "V0"), (V1, "V1"), (V2, "V2")):
            ADD("pool", (lambda e, t_=t_: e.memset(t_[:], 0.0)), [], [nm + "_all"])
        ADD("act", lambda e: e.activation(out=lrup[:, 0, :], in_=pvec[:, PV_LAM:PV_LAM + 8], func=AF.Exp, scale=-1.0),
            ["pvec"], ["lrup"])
        ADD("act", lambda e: e.activation(out=lrup[:, 0, :], in_=lrup[:, 0, :], func=AF.Ln, bias=1.0), ["lrup"], ["lrup"])
        ADD("dve", lambda e: e.tensor_scalar(out=lrup[:, 1, :], in0=lrup[:, 0, :], scalar1=-8.0, scalar2=0.0, op0=ALU.mult, op1=ALU.add),
            ["lrup"], ["lrup"])
        ADD("dve", lambda e: e.tensor_scalar(out=lrup[:, 0, :], in0=lrup[:, 1, :], scalar1=0.5, scalar2=0.0, op0=ALU.mult, op1=ALU.add),
            ["lrup"], ["lrup"])
        ADD("dve", lambda e: e.tensor_scalar(out=lrup[:, 2, :], in0=pvec[:, PV_BA:PV_BA + 8], scalar1=0.5, scalar2=0.0,
                                             op0=ALU.mult, op1=ALU.add), ["pvec", "lrup"], ["lrup"])
        ADD("dve", lambda e: e.tensor_scalar(out=lrup[:, 3, :], in0=pvec[:, PV_BX:PV_BX + 8], scalar1=0.5, scalar2=0.0,
                                             op0=ALU.mult, op1=ALU.add), ["pvec", "lrup"], ["lrup"])
        cact = ztmp[:, 0, 0:KC * nseq].rearrange("p (k b) -> p k b", k=KC)
        bb4 = ztmp[:, 1, :]
        ADD("sp", lambda e: e.dma_start(out=cact, in_=cT_d), [], ["zt0"], dma=True)
        ADD("act", lambda e: e.activation(out=cact, in_=cact, func=AF.Silu), ["zt0"], ["zt0"])
        stg = [av(i * 8192, 8192, F32).rearrange("p (k n) -> p k n", k=KC) for i in range(2)]
        mts = [av(16384 + i * 1024, 1024, F32) for i in range(2)]
        for blk in range(12):
            kind = blk // 2
            sst = stg[blk % 2]; sstn = "sstage%d" % (blk % 2)
            mt = mts[blk % 2]; mtn = "mt%d" % (blk % 2)
            ADD("pool", (lambda e, blk=blk, sst=sst: e.dma_start(out=sst, in_=wada_d[blk])), [], [sstn], dma=True)
            ADD("sp", (lambda e, blk=blk: e.dma_start(out=bb4[0:nseq, :],
                                                      in_=bada_d[0:1, blk * 512:(blk + 1) * 512].partition_broadcast(nseq))),
                [], ["zt1"], dma=True)
            pb_, pbn = bk(blk)
            for kc in range(KC):
                ADD("pe", (lambda e, kc=kc, pb_=pb_, sst=sst: e.matmul(pb_[0:nseq, :], lhsT=cact[:, kc, :], rhs=sst[:, kc, :],
                                                                       start=(kc == 0), stop=(kc == KC - 1))),
                    ["zt0", sstn], [pbn])
            ADD("dve", (lambda e, pb_=pb_, mt=mt: e.tensor_tensor(out=mt[0:nseq, :], in0=pb_[0:nseq, :], in1=bb4[0:nseq, :], op=ALU.add)),
                [pbn, "zt1"], [mtn])
            if kind in (2, 5):
                gi = 0 if kind == 2 else 1
                col = gi * 1024 + (blk % 2) * 512
                ADD("sp", (lambda e, col=col, mt=mt: e.dma_start(out=gsc_d[0:nseq, col:col + 512], in_=mt[0:nseq, :])),
                    [mtn], ["gsc"], dma=True)
            else:
                ki = {0: 0, 1: 1, 3: 2, 4: 3}[kind]
                if kind in (1, 4):
                    ADD("dve", (lambda e, mt=mt: e.tensor_scalar(out=mt[0:nseq, :], in0=mt[0:nseq, :], scalar1=1.0, scalar2=0.0,
                                                                 op0=ALU.add, op1=ALU.add)), [mtn], [mtn])
                tb_, tbn = bk(blk + 4)
                for q4 in range(4):
                    ADD("pe", (lambda e, q4=q4, tb_=tb_, mt=mt: e.transpose(out=tb_[:, q4 * nseq:(q4 + 1) * nseq],
                                                                            in_=mt[0:nseq, q4 * 128:(q4 + 1) * 128],
                                                                            identity=ident[0:nseq, 0:nseq])),
                        [mtn, "ident"], [tbn])
                c0 = ki * 8 + (blk % 2) * 4
                ADD("dve", (lambda e, c0=c0, tb_=tb_: e.tensor_copy(
                    out=modfm[:, c0:c0 + 4, :], in_=tb_[:, 0:4 * nseq].rearrange("p (c b) -> p c b", c=4))),
                    [tbn], ["modfm"])
        for b_ in range(NBLK):
            ADD("pool", (lambda e, b_=b_: e.dma_start(out=wsc_d[b_], in_=wall_d[b_])),
                ["wsc%d" % (b_ - 2)] if b_ >= 2 else [], ["wsc%d" % b_], dma=True)

        wstate = {"next": 0}
        total_blocks = nseq * NW * NBLK

        def ensure_loaded(g):
            lim = min(g + NWB, total_blocks)
            while wstate["next"] < lim:
                n = wstate["next"]
                slot = n % NWB
                blk = n % NBLK
                ADD("sp", (lambda e, slot=slot, blk=blk: e.dma_start(out=wbuf[:, slot], in_=wsc_d[blk])),
                    ["wsc%d" % blk], ["wb%d" % slot], dma=True)
                wstate["next"] += 1

        def wblk(g):
            ensure_loaded(g)
            slot = g % NWB
            return wbuf[:, slot], "wb%d" % slot

        pc = {"bank": 0}

        def xst_load(tok0, j):
            ADD("sp", (lambda e: e.dma_start(out=xst[:, j % 2, :], in_=x_d[tok0 + j * P:tok0 + (j + 1) * P, :])),
                [], ["xst%d" % (j % 2)], dma=True)

        def nbank():
            pc["bank"] += 1
            return bk(pc["bank"])

        gctr = 0
        for sq in range(nseq):
            ADD("sp", (lambda e, sq=sq: e.dma_start(out=g12b[:].rearrange("p a d -> p (a d)"),
                                                    in_=gsc_d[sq:sq + 1, :].partition_broadcast(P))),
                ["gsc"], ["g12b"], dma=True)
            ADD("pool", lambda e: e.memset(hist_l[:], 0.0), [], ["hist_l%d" % c_ for c_ in range(8)])
            ADD("pool", lambda e: e.memset(hstate[:], 0.0), [], ["hstate%d" % c_ for c_ in range(8)])
            ADD("pool", lambda e: e.memset(hist_f[:], 0.0), [], ["hist_f%d" % c_ for c_ in range(48)])
            for w in range(NW):
                tok0 = sq * S + w * T
                slot = w % 2
                pslot = 1 - slot
                if sq == 0 and w == 0:
                    xst_load(tok0, 0)
                    xst_load(tok0, 1)
                for pair in range(2):
                    if pair == 1:
                        xst_load(tok0, 2)
                        xst_load(tok0, 3)
                    for kc in range(KC):
                        pb_, pbn = nbank()
                        for t_ in range(2):
                            ADD("pe", (lambda e, kc=kc, t_=t_, pb_=pb_: e.transpose(
                                out=pb_[:, t_ * 128:(t_ + 1) * 128], in_=xst[:, t_, kc * 128:(kc + 1) * 128], identity=ident[:])),
                                ["xst%d" % t_, "ident"], [pbn])
                        ADD("act", (lambda e, kc=kc, pb_=pb_, sq=sq, pair=pair: e.activation(
                            out=uT[:, kc, pair * 256:(pair + 1) * 256], in_=pb_[:, 0:256], func=AF.Identity,
                            scale=modfm[:, 8 + kc, sq:sq + 1], bias=modfm[:, kc, sq:sq + 1])),
                            [pbn, "modfm"], ["uT%d" % kc])
                for j in range(4):
                    ADD("pool", (lambda e, tok0=tok0, j=j: e.dma_start(
                        out=x1tok[:, j, :], in_=x_d[tok0 + j * P:tok0 + (j + 1) * P, :])),
                        [], ["x1t%d" % j], dma=True)
                uTr = ["uT%d" % k for k in range(KC)]

                pg.handoff(ALL_A, M_BUFS)
                for g in (2,):
                    wb_, wbn = wblk(gctr); gctr += 1
                    if g < 2:
                        for j in range(4):
                            pb_, pbn = nbank()
                            for kc in range(KC):
                                lhs = uT[:, kc, j * 128:(j + 1) * 128] if g == 0 else uT[:, kc, j:T:4]
                                ADD("pe", (lambda e, kc=kc, lhs=lhs, pb_=pb_, wb_=wb_: e.matmul(
                                    pb_[:, :], lhsT=lhs, rhs=wb_[:, kc, :], start=(kc == 0), stop=(kc == KC - 1))),
                                    [wbn, "uT%d" % kc], [pbn])
                            dst = (V0 if g == 0 else V1)[:, slot, j, :]
                            ADD("dve", (lambda e, dst=dst, pb_=pb_: e.tensor_copy(out=dst, in_=pb_[:, :])),
                                [pbn, "V%d_all" % g], ["V%d_%d" % (g, slot)])
                    else:
                        for c4 in range(4):
                            pb_, pbn = nbank()
                            for kc in range(KC):
                                ADD("pe", (lambda e, kc=kc, c4=c4, pb_=pb_, wb_=wb_: e.matmul(
                                    pb_[:, :], lhsT=uT[:, kc, c4:T:4], rhs=wb_[:, kc, :],
                                    start=(kc == 0), stop=(kc == KC - 1))), [wbn, "uT%d" % kc], [pbn])
                            ADD("act", (lambda e, c4=c4, pb_=pb_: e.activation(out=a_v2st[:, c4, :], in_=pb_[:, :], func=AF.Copy)),
                                [pbn], ["v2st%d" % c4])
                            for sub in range(4):
                                ADD("pool", (lambda e, c4=c4, sub=sub, w=w: e.dma_start(
                                    out=V2[w * 32:(w + 1) * 32, c4 + 4 * sub, :], in_=a_v2st[sub:P:4, c4, :])),
                                    ["v2st%d" % c4, "V2_all"], ["V2"], dma=True)

                for blk in range(6):
                    wb_, wbn = wblk(gctr); gctr += 1
                    g = blk % 3
                    isq = blk < 3
                    for hp in range(HP):
                        pb_, pbn = nbank()
                        for kc in range(KC):
                            ADD("pe", (lambda e, kc=kc, hp=hp, pb_=pb_, wb_=wb_: e.matmul(
                                pb_[:, :], lhsT=wb_[:, kc, hp * 128:(hp + 1) * 128], rhs=uT[:, kc, :],
                                start=(kc == 0), stop=(kc == KC - 1))), [wbn, "uT%d" % kc], [pbn])
                        if g == 0:
                            src = pb_[:, :]
                            dst = a_qT[0][:, hp, :] if isq else K0[:, hp, slot, :]
                        elif g == 1:
                            src = pb_[:, :].rearrange("p (m c) -> p c m", c=4)
                            dst = (a_qT[1][:, hp, :] if isq else K1[:, hp, slot, :]).rearrange("p (c m) -> p c m", c=4)
                        else:
                            src = pb_[:, :].rearrange("p (m c) -> p c m", c=16)
                            dst = (a_qT[2][:, hp, :].rearrange("p (c m) -> p c m", c=16) if isq
                                   else K2[:, hp, :, w * 32:(w + 1) * 32])
                        wn = ("q%d_%d" % (g, hp)) if isq else ("K%d_%d" % (g, hp))
                        eng = "act" if isq else "dve"
                        if isq:
                            ADD("act", (lambda e, src=src, dst=dst: e.activation(out=dst, in_=src, func=AF.Copy)),
                                [pbn], [wn])
                        else:
                            ADD("dve", (lambda e, src=src, dst=dst: e.tensor_copy(out=dst, in_=src)),
                                [pbn, "K%d_all" % g], [wn])
                for g in (0, 1):
                    wb_, wbn = wblk(gctr); gctr += 1
                    if g < 2:
                        for j in range(4):
                            pb_, pbn = nbank()
                            for kc in range(KC):
                                lhs = uT[:, kc, j * 128:(j + 1) * 128] if g == 0 else uT[:, kc, j:T:4]
                                ADD("pe", (lambda e, kc=kc, lhs=lhs, pb_=pb_, wb_=wb_: e.matmul(
                                    pb_[:, :], lhsT=lhs, rhs=wb_[:, kc, :], start=(kc == 0), stop=(kc == KC - 1))),
                                    [wbn, "uT%d" % kc], [pbn])
                            dst = (V0 if g == 0 else V1)[:, slot, j, :]
                            ADD("dve", (lambda e, dst=dst, pb_=pb_: e.tensor_copy(out=dst, in_=pb_[:, :])),
                                [pbn, "V%d_all" % g], ["V%d_%d" % (g, slot)])
                    else:
                        for c4 in range(4):
                            pb_, pbn = nbank()
                            for kc in range(KC):
                                ADD("pe", (lambda e, kc=kc, c4=c4, pb_=pb_, wb_=wb_: e.matmul(
                                    pb_[:, :], lhsT=uT[:, kc, c4:T:4], rhs=wb_[:, kc, :],
                                    start=(kc == 0), stop=(kc == KC - 1))), [wbn, "uT%d" % kc], [pbn])
                            ADD("act", (lambda e, c4=c4, pb_=pb_: e.activation(out=a_v2st[:, c4, :], in_=pb_[:, :], func=AF.Copy)),
                                [pbn], ["v2st%d" % c4])
                            for sub in range(4):
                                ADD("pool", (lambda e, c4=c4, sub=sub, w=w: e.dma_start(
                                    out=V2[w * 32:(w + 1) * 32, c4 + 4 * sub, :], in_=a_v2st[sub:P:4, c4, :])),
                                    ["v2st%d" % c4, "V2_all"], ["V2"], dma=True)

                items = []
                rot = {"L": 0, "lt": 0, "pT": 0}
                NPT = 5

                def s1_common(mm_list, width, tb_ap, view, zero_prev):
                    L_, Ln = bk(rot["L"] % 4); rot["L"] += 1
                    lt = a_lt[rot["lt"] % 2]; ltn = "lt%d" % (rot["lt"] % 2); rot["lt"] += 1
                    pT = a_pT[rot["pT"] % NPT]; ptn = "pT%d" % (rot["pT"] % NPT); rot["pT"] += 1
                    for (c0, c1, lhs, rhs, rd) in mm_list:
                        ADD("pe", (lambda e, L_=L_, c0=c0, c1=c1, lhs=lhs, rhs=rhs: e.matmul(
                            L_[:, c0:c1], lhsT=lhs, rhs=rhs, start=True, stop=True, skip_group_check=True)), rd, [Ln])
                    if view is None:
                        o_ap, i_ap = lt[:, 0:width], L_[:, 0:width]
                    else:
                        o_ap = lt[:, 0:width].rearrange("p (r m) -> p r m", r=view)
                        i_ap = L_[:, 0:width].rearrange("p (r m) -> p r m", r=view)
                    ADD("dve", (lambda e, o_ap=o_ap, i_ap=i_ap, tb_ap=tb_ap: e.scalar_tensor_tensor(
                        out=o_ap, in0=i_ap, scalar=0.125, in1=tb_ap, op0=ALU.mult, op1=ALU.add)), [Ln, "tabs"], [ltn])
                    ADD("act", (lambda e, lt=lt, pT=pT, width=width: e.activation(
                        out=pT[:, 0:width], in_=lt[:, 0:width], func=AF.Exp)), [ltn], [ptn])
                    if zero_prev:
                        ADD("act", (lambda e, pT=pT: e.activation(out=pT[:, 0:128], in_=pT[:, 0:128],
                                                                  func=AF.Copy, scale=0.0)), [ptn], [ptn])
                    return pT, ptn

                for hp in range(HP):
                    X_, Xn = bk(4 + 2 * (hp % 2))
                    S_, Sn = bk(5 + 2 * (hp % 2))
                    for h2 in range(2):
                        h = hp * 2 + h2
                        rows = slice(h2 * 64, h2 * 64 + 64)
                        first = [True]

                        def pv(vt, vreads, pT_ap, ptn, ocols, h=h, rows=rows, first=first, X_=X_, S_=S_, Xn=Xn, Sn=Sn):
                            st = first[0]
                            first[0] = False
                            ADD("pe", (lambda e: e.matmul(X_[rows, ocols], lhsT=vt[:, h * 64:(h + 1) * 64], rhs=pT_ap,
                                                          start=st, stop=True, skip_group_check=True)),
                                vreads + [ptn], [Xn])
                            ADD("pe", (lambda e: e.matmul(S_[rows, ocols], lhsT=ones_bf[:, :], rhs=pT_ap,
                                                          start=st, stop=True, skip_group_check=True)),
                                ["ones", ptn], [Sn])

                        for g in range(2):
                            Kt = K0 if g == 0 else K1
                            Vt = V0 if g == 0 else V1
                            for j in range(4):
                                qap = a_qT[g][rows, hp, j * 128:(j + 1) * 128]
                                if g == 0:
                                    kprev = Kt[rows, hp, pslot, 384:512] if j == 0 else Kt[rows, hp, slot, (j - 1) * 128:j * 128]
                                    vprev = Vt[:, pslot, 3, :] if j == 0 else Vt[:, slot, j - 1, :]
                                    vpn = "V0_%d" % (pslot if j == 0 else slot)
                                    ocols = slice(j * 128, (j + 1) * 128)
                                else:
                                    kprev = Kt[rows, hp, pslot, j * 128:(j + 1) * 128]
                                    vprev = Vt[:, pslot, j, :]
                                    vpn = "V1_%d" % pslot
                                    ocols = slice(j, T, 4)
                                kcur = Kt[rows, hp, slot, j * 128:(j + 1) * 128]
                                vcur = Vt[:, slot, j, :]
                                kn = "K%d_%d" % (g, hp)
                                qn = "q%d_%d" % (g, hp)
                                tb = tabs[:, (g * 16 + h * 2):(g * 16 + h * 2 + 2), :].rearrange("p a q -> p (a q)")
                                mm = [(0, 128, kprev, qap, [kn, qn, "K%d_all" % g]), (128, 256, kcur, qap, [kn, qn])]
                                zp = (w == 0 and (g == 1 or j == 0))

                                def st1(mm=mm, tb=tb, zp=zp):
                                    return s1_common(mm, 256, tb, None, zp)

                                def st2(r_, pv=pv, vprev=vprev, vcur=vcur, vpn=vpn, g=g, ocols=ocols):
                                    pT, ptn = r_
                                    pv(vprev, [vpn, "V%d_all" % g], pT[:, 0:128], ptn, ocols)
                                    pv(vcur, ["V%d_%d" % (g, slot)], pT[:, 128:256], ptn, ocols)
                                items.append([st1, st2])
                        mm = [(r * 32, (r + 1) * 32, K2[rows, hp, r, :], a_qT[2][rows, hp, r * 32:(r + 1) * 32],
                               ["K2_%d" % hp, "q2_%d" % hp, "K2_all"]) for r in range(16)]
                        tsl = tabs[:, 32 + h, w * 32:(w + 1) * 32]
                        tb2 = bass.AP(tensor=tsl.tensor, offset=tsl.offset, ap=[list(tsl.ap[0]), [0, 16], [1, 32]])

                        def st1(mm=mm, tb2=tb2):
                            return s1_common(mm, 512, tb2, 16, False)

                        def st2(r_, pv=pv):
                            pT, ptn = r_
                            for r in range(16):
                                pv(V2[:, r, :], ["V2", "V2_all"], pT[:, r * 32:(r + 1) * 32], ptn, slice(r, T, 16))
                        items.append([st1, st2])

                    def fin(r_, prev=items[-1][1], S_=S_, X_=X_, Sn=Sn, Xn=Xn, hp=hp):
                        prev(r_)
                        ADD("dve", (lambda e: e.reciprocal(out=a_rec, in_=S_[:, :])), [Sn], ["rec"])
                        ADD("dve", (lambda e: e.tensor_tensor(out=oaT[:, hp, :], in0=X_[:, :], in1=a_rec, op=ALU.mult)),
                            [Xn, "rec"], ["oaT%d" % hp])
                    items[-1][1] = fin
                LA = 3
                nit = len(items)
                resv = [None] * nit
                for i in range(nit + LA):
                    if i < nit:
                        resv[i] = items[i][0]()
                    if i - LA >= 0:
                        items[i - LA][1](resv[i - LA])

                pg.handoff(ALL_A, L_BUFS)

                def nbank():
                    pc["bank"] += 1
                    return bk(pc["bank"])
                for cp in range(4):
                    wb_, wbn = wblk(gctr); gctr += 1
                    cs = (2 * cp, 2 * cp + 1)
                    Ab, Bb, Cb, Db = {}, {}, {}, {}
                    TS = {}
                    for c in cs:
                        cc = c % 2
                        Ab[c] = nbank()
                        Bb[c] = nbank()
                        TS[c] = (l_f32[cc * 7:(cc + 1) * 7], ["%s%d" % (n_, cc) for n_ in ("xr", "th_r", "th_i", "a_", "s_", "hh", "gel")],
                                 l_xrb[cc], "xrb%d" % cc)
                        A_, An = Ab[c]
                        B_, Bn = Bb[c]
                        for kc in range(KC):
                            ADD("pe", (lambda e, kc=kc, A_=A_, wb_=wb_, cc=cc: e.matmul(
                                A_[:, :], lhsT=wb_[:, kc, cc * 128:(cc + 1) * 128], rhs=uT[:, kc, :],
                                start=(kc == 0), stop=(kc == KC - 1))), [wbn, "uT%d" % kc], [An])
                        for kc in range(KC):
                            ADD("pe", (lambda e, kc=kc, B_=B_, wb_=wb_, cc=cc: e.matmul(
                                B_[:, :], lhsT=wb_[:, kc, (2 + cc) * 128:(3 + cc) * 128], rhs=uT[:, kc, :],
                                start=(kc == 0), stop=(kc == KC - 1))), [wbn, "uT%d" % kc], [Bn])
                    lcwf = lambda c, k: pvec[:, PV_LCW + c * 4 + k:PV_LCW + c * 4 + k + 1]
                    for c in cs:
                        (xr, th_r, th_i, a_, s_, hh, gel), (xrn, thrn, thin, an, sn, hhn, geln), xrb, xrbn = TS[c]
                        A_, An = Ab[c]
                        ADD("act", (lambda e, A_=A_, xr=xr, w3=lcwf(c, 3), bb_=pvec[:, PV_LCB + c:PV_LCB + c + 1]: e.activation(
                            out=xr, in_=A_[:, :], func=AF.Identity, scale=w3, bias=bb_)), [An, "pvec"], [xrn])
                    for c in cs:
                        (xr, th_r, th_i, a_, s_, hh, gel), (xrn, thrn, thin, an, sn, hhn, geln), xrb, xrbn = TS[c]
                        A_, An = Ab[c]
                        for k in range(3):
                            sh = 3 - k
                            ADD("dve", (lambda e, xr=xr, wk=lcwf(c, k), sh=sh, c=c: e.scalar_tensor_tensor(
                                out=xr[:, 0:sh], in0=hist_l[:, c, 3 - sh:3], scalar=wk, in1=xr[:, 0:sh],
                                op0=ALU.mult, op1=ALU.add)), ["hist_l%d" % c, "pvec", xrn], [xrn])
                        for k in range(3):
                            sh = 3 - k
                            ADD("dve", (lambda e, A_=A_, xr=xr, wk=lcwf(c, k), sh=sh: e.scalar_tensor_tensor(
                                out=xr[:, sh:T], in0=A_[:, 0:T - sh], scalar=wk, in1=xr[:, sh:T], op0=ALU.mult, op1=ALU.add)),
                                [An, "pvec", xrn], [xrn])
                    for c in cs:
                        (xr, th_r, th_i, a_, s_, hh, gel), (xrn, thrn, thin, an, sn, hhn, geln), xrb, xrbn = TS[c]
                        B_, Bn = Bb[c]
                        ADD("act", (lambda e, B_=B_, gel=gel: e.activation(out=gel, in_=B_[:, :], func=AF.Gelu_apprx_tanh)),
                            [Bn], [geln])
                    for c in cs:
                        (xr, th_r, th_i, a_, s_, hh, gel), (xrn, thrn, thin, an, sn, hhn, geln), xrb, xrbn = TS[c]
                        A_, An = Ab[c]
                        ADD("dve", (lambda e, A_=A_, c=c: e.tensor_copy(out=hist_l[:, c, :], in_=A_[:, T - 3:T])),
                            [An], ["hist_l%d" % c])
                        ADD("pool", (lambda e, xr=xr, xrb=xrb: e.tensor_copy(out=xrb, in_=xr)), [xrn], [xrbn])
                    for c in cs:
                        (xr, th_r, th_i, a_, s_, hh, gel), (xrn, thrn, thin, an, sn, hhn, geln), xrb, xrbn = TS[c]
                        Cb[c] = nbank()
                        Db[c] = nbank()
                        C_, Cn = Cb[c]
                        D_, Dn = Db[c]
                        ADD("pe", (lambda e, C_=C_, c=c, xrb=xrb: e.matmul(C_[:, :], lhsT=lruw[:, c, 0, :], rhs=xrb, start=True, stop=True)),
                            ["lruw", xrbn], [Cn])
                        ADD("pe", (lambda e, D_=D_, c=c, xrb=xrb: e.matmul(D_[:, :], lhsT=lruw[:, c, 1, :], rhs=xrb, start=True, stop=True)),
                            ["lruw", xrbn], [Dn])
                    for c in cs:
                        (xr, th_r, th_i, a_, s_, hh, gel), (xrn, thrn, thin, an, sn, hhn, geln), xrb, xrbn = TS[c]
                        C_, Cn = Cb[c]
                        D_, Dn = Db[c]
                        ADD("act", (lambda e, C_=C_, c=c, th_r=th_r: e.activation(out=th_r, in_=C_[:, :], func=AF.Tanh, scale=0.5,
                                                                                 bias=lrup[:, 2, c:c + 1])), [Cn, "lrup"], [thrn])
                        ADD("act", (lambda e, D_=D_, c=c, th_i=th_i: e.activation(out=th_i, in_=D_[:, :], func=AF.Tanh, scale=0.5,
                                                                                 bias=lrup[:, 3, c:c + 1])), [Dn, "lrup"], [thin])
                    for c in cs:
                        (xr, th_r, th_i, a_, s_, hh, gel), (xrn, thrn, thin, an, sn, hhn, geln), xrb, xrbn = TS[c]
                        ADD("act", (lambda e, c=c, a_=a_, th_r=th_r: e.activation(out=a_, in_=th_r, func=AF.Exp, scale=lrup[:, 0, c:c + 1],
                                                                                 bias=lrup[:, 0, c:c + 1])), [thrn, "lrup"], [an])
                        ADD("act", (lambda e, c=c, s_=s_, th_r=th_r: e.activation(out=s_, in_=th_r, func=AF.Exp, scale=lrup[:, 1, c:c + 1],
                                                                                 bias=lrup[:, 1, c:c + 1])), [thrn, "lrup"], [sn])
                    for c in cs:
                        (xr, th_r, th_i, a_, s_, hh, gel), (xrn, thrn, thin, an, sn, hhn, geln), xrb, xrbn = TS[c]
                        ADD("act", (lambda e, s_=s_: e.activation(out=s_, in_=s_, func=AF.Sqrt, scale=-1.0, bias=1.0)), [sn], [sn])
                    for c in cs:
                        (xr, th_r, th_i, a_, s_, hh, gel), (xrn, thrn, thin, an, sn, hhn, geln), xrb, xrbn = TS[c]
                        ADD("dve", (lambda e, th_i=th_i, xr=xr: e.scalar_tensor_tensor(
                            out=th_i, in0=th_i, scalar=1.0, in1=xr, op0=ALU.add, op1=ALU.mult)), [thin, xrn], [thin])
                        ADD("dve", (lambda e, th_i=th_i, s_=s_: e.scalar_tensor_tensor(
                            out=th_i, in0=th_i, scalar=0.5, in1=s_, op0=ALU.mult, op1=ALU.mult)), [thin, sn], [thin])
                        ADD("dve", (lambda e, c=c, hh=hh, a_=a_, th_i=th_i: e.tensor_tensor_scan(
                            out=hh, data0=a_, data1=th_i, initial=hstate[:, c:c + 1], op0=ALU.mult, op1=ALU.add)),
                            [an, thin, "hstate%d" % c], [hhn])
                        ADD("dve", (lambda e, c=c, hh=hh: e.tensor_copy(out=hstate[:, c:c + 1], in_=hh[:, T - 1:T])),
                            [hhn], ["hstate%d" % c])
                        ADD("dve", (lambda e, c=c, hh=hh, gel=gel: e.tensor_tensor(out=l_hgT[:, c, :], in0=hh, in1=gel, op=ALU.mult)),
                            [hhn, geln], ["hgT%d" % c])

                pg.handoff(ALL_A, E_BUFS)
                for c in range(KC):
                    wb_, wbn = wblk(gctr); gctr += 1
                    GA_, GAn = nbank()
                    GR_, GRn = nbank()
                    YA_, YAn = nbank()
                    YL_, YLn = nbank()
                    for kc in range(KC):
                        ADD("pe", (lambda e, kc=kc, GA_=GA_, wb_=wb_: e.matmul(
                            GA_[:, :], lhsT=wb_[:, kc, 0:128], rhs=uT[:, kc, :], start=(kc == 0), stop=(kc == KC - 1))),
                            [wbn, "uT%d" % kc], [GAn])
                    for kc in range(KC):
                        ADD("pe", (lambda e, kc=kc, GR_=GR_, wb_=wb_: e.matmul(
                            GR_[:, :], lhsT=wb_[:, kc, 128:256], rhs=uT[:, kc, :], start=(kc == 0), stop=(kc == KC - 1))),
                            [wbn, "uT%d" % kc], [GRn])
                    for kc in range(KC):
                        ADD("pe", (lambda e, kc=kc, YL_=YL_, wb_=wb_: e.matmul(
                            YL_[:, :], lhsT=wb_[:, kc, 256:384], rhs=l_hgT[:, kc, :], start=(kc == 0), stop=(kc == KC - 1))),
                            [wbn, "hgT%d" % kc], [YLn])
                    for kc in range(4):
                        ADD("pe", (lambda e, kc=kc, YA_=YA_, wb_=wb_: e.matmul(
                            YA_[:, :], lhsT=wb_[:, kc, 384:512], rhs=oaT[:, kc, :], start=(kc == 0), stop=(kc == 3))),
                            [wbn, "oaT%d" % kc], [YAn])
                    sA, sR, m1 = m_sig
                    ADD("act", (lambda e, GA_=GA_: e.activation(out=sA, in_=GA_[:, :], func=AF.Sigmoid)), [GAn], ["sA"])
                    ADD("act", (lambda e, GR_=GR_: e.activation(out=sR, in_=GR_[:, :], func=AF.Sigmoid)), [GRn], ["sR"])
                    ADD("dve", (lambda e, YA_=YA_: e.tensor_tensor(out=sA, in0=YA_[:, :], in1=sA, op=ALU.mult)), [YAn, "sA"], ["sA"])
                    ADD("dve", (lambda e, YL_=YL_: e.tensor_tensor(out=sR, in0=YL_[:, :], in1=sR, op=ALU.mult)), [YLn, "sR"], ["sR"])
                    ADD("dve", (lambda e, c=c: e.tensor_tensor(out=m_mT[:, c, :], in0=sA, in1=sR, op=ALU.add)),
                        ["sA", "sR"], ["mT%d" % c])

                def layernorm(j, gi):
                    for hf in range(2):
                        ADD("dve", (lambda e, hf=hf: e.bn_stats(out=stats[:, hf, :], in_=x1tok[:, j, hf * 512:(hf + 1) * 512])),
                            ["x1t%d" % j], ["stats"])
                    ADD("dve", lambda e: e.bn_aggr(out=mv[:, 0:2], in_=stats[:].rearrange("p a b -> p (a b)")), ["stats"], ["mv"])
                    ADD("act", lambda e: e.activation(out=mv[:, 2:3], in_=mv[:, 1:2], func=AF.Sqrt, bias=LN_EPS), ["mv"], ["mv"])
                    ADD("dve", lambda e: e.reciprocal(out=mv[:, 3:4], in_=mv[:, 2:3]), ["mv"], ["mv"])
                    ADD("dve", lambda e: e.scalar_tensor_tensor(out=x1tok[:, j, :], in0=x1tok[:, j, :], scalar=mv[:, 0:1],
                                                                in1=lnbc[:, gi, :], op0=ALU.subtract, op1=ALU.mult),
                        ["x1t%d" % j, "mv", "lnbc"], ["x1t%d" % j])
                    ADD("dve", lambda e: e.scalar_tensor_tensor(out=x1tok[:, j, :], in0=x1tok[:, j, :], scalar=mv[:, 3:4],
                                                                in1=lnbc[:, gi + 1, :], op0=ALU.mult, op1=ALU.add),
                        ["x1t%d" % j, "mv", "lnbc"], ["x1t%d" % j])

                wbs = [wblk(gctr)]
                wbs.append((wbuf[:, (gctr + 1) % NWB], "wb%d" % ((gctr + 1) % NWB)))
                gctr += 2
                for j in range(4):
                    for nb in range(2):
                        wb_, wbn = wbs[nb]
                        pb_, pbn = nbank()
                        for kc in range(KC):
                            ADD("pe", (lambda e, kc=kc, j=j, pb_=pb_, wb_=wb_: e.matmul(
                                pb_[:, :], lhsT=m_mT[:, kc, j * 128:(j + 1) * 128], rhs=wb_[:, kc, :],
                                start=(kc == 0), stop=(kc == KC - 1))), [wbn, "mT%d" % kc], [pbn])
                        zt = ztmp[:, nb, :]
                        ztn = "zt%d" % nb
                        ADD("dve", (lambda e, pb_=pb_, zt=zt, nb=nb: e.tensor_tensor(
                            out=zt, in0=pb_[:, :], in1=g12b[:, 0, nb * 512:(nb + 1) * 512], op=ALU.mult)),
                            [pbn, "g12b"], [ztn])
                        xs = x1tok[:, j, nb * 512:(nb + 1) * 512]
                        ADD("dve", (lambda e, xs=xs, zt=zt: e.scalar_tensor_tensor(
                            out=xs, in0=xs, scalar=ALPHA, in1=zt, op0=ALU.mult, op1=ALU.add)), [ztn, "x1t%d" % j], ["x1t%d" % j])
                    layernorm(j, 0)
                for pair in range(2):
                    for kc in range(KC):
                        pb_, pbn = nbank()
                        for t_ in range(2):
                            j = pair * 2 + t_
                            ADD("pe", (lambda e, kc=kc, j=j, t_=t_, pb_=pb_: e.transpose(
                                out=pb_[:, t_ * 128:(t_ + 1) * 128], in_=x1tok[:, j, kc * 128:(kc + 1) * 128], identity=ident[:])),
                                ["x1t%d" % j, "ident"], [pbn])
                        ADD("act", (lambda e, kc=kc, pb_=pb_, sq=sq, pair=pair: e.activation(
                            out=uT[:, kc, pair * 256:(pair + 1) * 256], in_=pb_[:, 0:256], func=AF.Identity,
                            scale=modfm[:, 24 + kc, sq:sq + 1], bias=modfm[:, 16 + kc, sq:sq + 1])),
                            [pbn, "modfm"], ["uT%d" % kc])

                pg.handoff(ALL_A, F_BUFS)
                if not (sq == nseq - 1 and w == NW - 1):
                    xst_load(tok0 + T, 0)
                    xst_load(tok0 + T, 1)
                hfn = ["hist_f%d" % c_ for c_ in range(48)]
                Wk = lambda k: pvec[:, PV_FCW + k:PV_FCW + 144:3]
                ADD("dve", lambda e: e.tensor_tensor(out=corr[:, :, 1], in0=Wk(0), in1=hist_f[:, :, 1], op=ALU.mult),
                    hfn + ["pvec"], ["corr"])
                ADD("dve", lambda e: e.tensor_tensor(out=corr[:, :, 0], in0=Wk(0), in1=hist_f[:, :, 0], op=ALU.mult),
                    hfn + ["pvec"], ["corr"])
                ADD("dve", lambda e: e.tensor_tensor(out=ctmp[:, :], in0=Wk(1), in1=hist_f[:, :, 1], op=ALU.mult),
                    hfn + ["pvec"], ["ctmp"])
                ADD("dve", lambda e: e.tensor_tensor(out=corr[:, :, 0], in0=corr[:, :, 0], in1=ctmp[:, :], op=ALU.add),
                    ["corr", "ctmp"], ["corr"])
                fstate = {}

                def ffn_a(pr):
                    if pr % 2 == 0:
                        fstate["wb"] = wblk(fstate["g"]); fstate["g"] += 1
                    wb_, wbn = fstate["wb"]
                    pc2 = pr % 2
                    A_, An = nbank()
                    B_, Bn = nbank()
                    for kc in range(KC):
                        ADD("pe", (lambda e, kc=kc: e.matmul(
                            A_[:, :], lhsT=wb_[:, kc, pc2 * 128:(pc2 + 1) * 128], rhs=uT[:, kc, :],
                            start=(kc == 0), stop=(kc == KC - 1))), [wbn, "uT%d" % kc], [An])
                    for kc in range(KC):
                        ADD("pe", (lambda e, kc=kc: e.matmul(
                            B_[:, :], lhsT=wb_[:, kc, (2 + pc2) * 128:(3 + pc2) * 128], rhs=uT[:, kc, :],
                            start=(kc == 0), stop=(kc == KC - 1))), [wbn, "uT%d" % kc], [Bn])
                    fs = pr % 2
                    cv, cg, gl = f_tmp[3 * fs:3 * fs + 3]
                    cvn, cgn, gln = "cv%d" % fs, "cg%d" % fs, "gl%d" % fs
                    halves = ((A_, An, pr, cv, cvn), (B_, Bn, 24 + pr, cg, cgn))
                    for (PB, PBn, ch, dst, dn) in halves:
                        ADD("act", (lambda e, PB=PB, ch=ch, dst=dst: e.activation(
                            out=dst, in_=PB[:, :], func=AF.Identity, scale=pvec[:, PV_FCW + ch * 3 + 2:PV_FCW + ch * 3 + 3],
                            bias=pvec[:, PV_FCB + ch:PV_FCB + ch + 1])), [PBn, "pvec"], [dn])
                    for (PB, PBn, ch, dst, dn) in halves:
                        ADD("dve", (lambda e, ch=ch, dst=dst: e.tensor_tensor(
                            out=dst[:, 0:2], in0=dst[:, 0:2], in1=corr[:, ch, :], op=ALU.add)), ["corr", dn], [dn])
                    for (PB, PBn, ch, dst, dn) in halves:
                        for k in range(2):
                            sh = 2 - k
                            ADD("dve", (lambda e, PB=PB, dst=dst, ch=ch, k=k, sh=sh: e.scalar_tensor_tensor(
                                out=dst[:, sh:T], in0=PB[:, 0:T - sh], scalar=pvec[:, PV_FCW + ch * 3 + k:PV_FCW + ch * 3 + k + 1],
                                in1=dst[:, sh:T], op0=ALU.mult, op1=ALU.add)), [PBn, "pvec", dn], [dn])
                    return (halves, cv, cg, gl, cvn, cgn, gln, pr)

                def ffn_b(st):
                    halves, cv, cg, gl, cvn, cgn, gln, pr = st
                    for (PB, PBn, ch, dst, dn) in halves:
                        ADD("act", (lambda e, PB=PB, ch=ch: e.activation(out=hist_f[:, ch, :], in_=PB[:, T - 2:T], func=AF.Copy)),
                            [PBn, "corr"], ["hist_f%d" % ch])
                    ADD("act", (lambda e: e.activation(out=gl, in_=cg, func=AF.Gelu_apprx_tanh)), [cgn], [gln])
                    ADD("pool", (lambda e: e.tensor_tensor(out=f_gT[:, pr, :], in0=gl, in1=cv, op=ALU.mult)),
                        [gln, cvn], ["gT%d" % pr])

                fstate["g"] = gctr
                fprev = None
                for pr in range(25):
                    cur = ffn_a(pr) if pr < 24 else None
                    if fprev is not None:
                        ffn_b(fprev)
                    fprev = cur
                gctr = fstate["g"]

                for nb in range(2):
                    accs = [nbank() for _ in range(4)]
                    for kg in range(3):
                        wb_, wbn = wblk(gctr); gctr += 1
                        for j in range(4):
                            pb_, pbn = accs[j]
                            for kc in range(KC):
                                ADD("pe", (lambda e, kc=kc, j=j, kg=kg, pb_=pb_, wb_=wb_: e.matmul(
                                    pb_[:, :], lhsT=f_gT[:, kg * 8 + kc, j * 128:(j + 1) * 128], rhs=wb_[:, kc, :],
                                    start=(kg == 0 and kc == 0), stop=(kg == 2 and kc == KC - 1))), [wbn, "gT%d" % (kg * 8 + kc)], [pbn])
                    for j in range(4):
                        pb_, pbn = accs[j]
                        zt = ztmp[:, j % 2, :]
                        ztn = "zt%d" % (j % 2)
                        ADD("dve", (lambda e, pb_=pb_, zt=zt, nb=nb: e.tensor_tensor(
                            out=zt, in0=pb_[:, :], in1=g12b[:, 1, nb * 512:(nb + 1) * 512], op=ALU.mult)),
                            [pbn, "g12b"], [ztn])
                        xs = x1tok[:, j, nb * 512:(nb + 1) * 512]
                        ADD("dve", (lambda e, xs=xs, zt=zt: e.scalar_tensor_tensor(
                            out=xs, in0=xs, scalar=ALPHA, in1=zt, op0=ALU.mult, op1=ALU.add)), [ztn, "x1t%d" % j], ["x1t%d" % j])
                for j in range(4):
                    layernorm(j, 2)
                    ADD("pool", (lambda e, tok0=tok0, j=j: e.dma_start(
                        out=y_d[tok0 + j * P:tok0 + (j + 1) * P, :], in_=x1tok[:, j, :])),
                        ["x1t%d" % j], ["ystore"], dma=True)
        last = pg.add("sp", None, ["ystore"], [])
        last.deps = sorted(set(last.deps) | {o.idx for o in pg.dsem_last.values()})
        pg.emit(nc, es)
    return nc


def _rel_bucket(dist):
    dist = dist.astype(np.int32)
    nf = np.maximum(dist, 1).astype(np.float32)
    large = 16 + (np.log(nf / np.float32(16)) / np.float32(math.log(2048 / 16)) * np.float32(16)).astype(np.int32)
    large = np.minimum(large, 31)
    return np.where(dist < 16, dist, large)


def _blk(w):
    return np.ascontiguousarray(w.reshape(KC, P, w.shape[1]).transpose(1, 0, 2))


def prepare_shared(inp):
    f = lambda k: np.asarray(inp[k], dtype=np.float32)
    w_in = f("w_in")[0]
    wall = np.zeros((NBLK, P, KC, 512), np.float32)
    for pos, i in enumerate((8, 0, 1, 2, 3, 4, 5, 6, 7)):
        wall[pos] = _blk(w_in[:, 512 * i:512 * (i + 1)])
    XL, LG, GA, GR = 4608, 5632, 6656, 7680
    for i in range(4):
        cols = np.concatenate([w_in[:, XL + 256 * i:XL + 256 * (i + 1)], w_in[:, LG + 256 * i:LG + 256 * (i + 1)]], 1)
        wall[9 + i] = _blk(cols)
    wpa = f("w_proj_attn")[0]
    wpl = f("w_proj_lru")[0]
    for c in range(8):
        pa = np.zeros((1024, 128), np.float32)
        pa[0:512] = wpa[:, 128 * c:128 * (c + 1)]
        cols = np.concatenate([w_in[:, GA + 128 * c:GA + 128 * (c + 1)], w_in[:, GR + 128 * c:GR + 128 * (c + 1)],
                               wpl[:, 128 * c:128 * (c + 1)], pa], 1)
        wall[13 + c] = _blk(cols)
    wo = f("w_out")[0]
    for nb in range(2):
        wall[21 + nb] = _blk(wo[:, 512 * nb:512 * (nb + 1)])
    wu = f("ffn_w_up")[0]
    for i in range(12):
        cols = np.concatenate([wu[:, 256 * i:256 * (i + 1)], wu[:, 3072 + 256 * i:3072 + 256 * (i + 1)]], 1)
        wall[23 + i] = _blk(cols)
    wd = f("ffn_w_down")[0]
    for nb in range(2):
        for kg in range(3):
            wall[35 + nb * 3 + kg] = _blk(wd[kg * 1024:(kg + 1) * 1024, 512 * nb:512 * (nb + 1)])
    wada = f("w_ada")[0]
    wada_b = np.stack([_blk(wada[:, 512 * i:512 * (i + 1)]) for i in range(12)])
    bada = f("b_ada").reshape(1, 6144)
    rb = f("rel_bias")
    tabs = np.zeros((P, 40, 128), np.float32)
    k = np.arange(128)[:, None]
    q = np.arange(128)[None, :]
    for g, dil in enumerate((1, 4, 16)):
        dprev = q + 128 - k
        dcur = q - k
        bprev = _rel_bucket(np.maximum(dprev, 0) * dil)
        bcur = _rel_bucket(np.maximum(dcur, 0) * dil)
        for h in range(8):
            col = rb[:, g * 8 + h]
            tp = np.where(dprev <= 128, col[bprev], np.float32(NEG))
            tc = np.where(dcur >= 0, col[bcur], np.float32(NEG))
            if g < 2:
                tabs[:, g * 16 + h * 2] = tp
                tabs[:, g * 16 + h * 2 + 1] = tc
            else:
                tabs[:, 32 + h] = tc
    pvec = np.zeros((P, PV_N), np.float32)
    fm = lambda v: np.ascontiguousarray(v.reshape(-1, P).T)
    lcw = f("lru_conv_w")[0]
    pvec[:, PV_LCW:PV_LCW + 32] = np.stack([fm(lcw[kk]) for kk in range(4)], 2).reshape(P, 32)
    pvec[:, PV_LCB:PV_LCB + 8] = fm(f("lru_conv_b")[0])
    pvec[:, PV_BA:PV_BA + 8] = fm(f("lru_ba")[0])
    pvec[:, PV_BX:PV_BX + 8] = fm(f("lru_bx")[0])
    pvec[:, PV_LAM:PV_LAM + 8] = fm(f("lru_lambda")[0])
    fcw = f("ffn_conv_w")[0]
    pvec[:, PV_FCW:PV_FCW + 144] = np.stack([fm(fcw[kk]) for kk in range(3)], 2).reshape(P, 144)
    pvec[:, PV_FCB:PV_FCB + 48] = fm(f("ffn_conv_b")[0])
    lruw = np.zeros((P, 8, 2, 128), np.float32)
    for gi, nm in enumerate(("lru_wa", "lru_wx")):
        wg = f(nm)[0]
        for n in range(16):
            c, o = n // 2, (n % 2) * 64
            lruw[o:o + 64, c, gi, o:o + 64] = wg[n]
    lnp = np.stack([f("ln1_g")[0], f("ln1_b")[0], f("ln2_g")[0], f("ln2_b")[0]]).reshape(1, 4 * D)
    return {"wada": wada_b, "bada": bada, "wall": wall, "tabs": tabs.reshape(P, 40 * 128), "pvec": pvec,
            "lruw": lruw.reshape(P, 8 * 2 * 128), "lnp": lnp}


def core_inputs(inp, shared, b0, nseq):
    x = np.asarray(inp["x"], dtype=np.float32)[b0:b0 + nseq].reshape(nseq * S, D)
    c = np.asarray(inp["c"], dtype=np.float32)[b0:b0 + nseq]
    cT = np.ascontiguousarray(c.reshape(nseq, KC, P).transpose(2, 1, 0))
    m = dict(shared)
    m["x"] = np.ascontiguousarray(x)
    m["cT"] = cT
    return m


_NC_CACHE = {}


def kernel(**inputs):
    nseq = SEQ_PER_CORE
    if nseq not in _NC_CACHE:
        _NC_CACHE[nseq] = build_program(nseq)
    nc = _NC_CACHE[nseq]
    shared = prepare_shared(inputs)
    in_maps = [core_inputs(inputs, shared, i * nseq, nseq) for i in range(N_CORES)]
    res = run_bass_kernel_spmd(nc, in_maps, core_ids=list(range(N_CORES)))
    outs = [np.asarray(r["y"]).reshape(nseq, S, D) for r in res.results]
    return np.concatenate(outs, axis=0).astype(np.float32)
```
